# Optimizing a Trainium2 kernel written in Bass

```python
import jax, jax.numpy as jnp
from jax import lax
import numpy as np

D_MODEL = 1024
BATCH = 8
SEQ = 4096
DEPTH = 1
DEC_BATCH = 8
DEC_SEQ = 64
PAST_LEN = 1024

CHUNK = 64
N_META = 16
Q_BLOCK = 128
FOX_HEADS = 8
FOX_HEAD_DIM = 64
FOX_WIDTH = FOX_HEADS * FOX_HEAD_DIM
ML_HEADS = 4
ML_HEAD_DIM = 128
ML_WIDTH = ML_HEADS * ML_HEAD_DIM
CONV_WIDTH = 4
D_FF = ((-(-8 * D_MODEL // 3)) + 255) // 256 * 256
RMS_EPS = 1e-6
PAD_LOG_GATE = -1e30
SPLITS = (FOX_WIDTH, FOX_WIDTH, FOX_WIDTH, FOX_HEADS, ML_WIDTH, ML_WIDTH, ML_WIDTH, ML_HEADS, ML_HEADS, D_MODEL, D_MODEL)
SPLIT_IDX = tuple(int(i) for i in np.cumsum(SPLITS)[:-1])
D_IN = sum(SPLITS)

kernel_name = 'hybrid_fox_mlstm_stream_step'


def rmsnorm(x, g):
    xf = x.astype(jnp.float32)
    y = xf * lax.rsqrt(jnp.mean(xf * xf, axis=-1, keepdims=True) + RMS_EPS)
    return (y * g.astype(jnp.float32)).astype(x.dtype)


def causal_conv(x, prev, w, b):
    T = x.shape[1]
    xc = jnp.concatenate([prev.astype(x.dtype), x], axis=1)
    y = b + sum(xc[:, i:i + T] * w[i] for i in range(CONV_WIDTH))
    return y, xc[:, T:]


def fox_block(q, Fq, qpos, k, v, Fk):
    s = jnp.einsum('bqhd,bkhd->bhqk', q, k).astype(jnp.float32) * (FOX_HEAD_DIM ** -0.5)
    bias = jnp.transpose(Fq, (0, 2, 1))[:, :, :, None] - jnp.transpose(Fk, (0, 2, 1))[:, :, None, :]
    mask = jnp.arange(k.shape[1])[None, :] <= qpos[:, None]
    s = jnp.where(mask[None, None], s + bias, -jnp.inf)
    p = jax.nn.softmax(s, axis=-1).astype(v.dtype)
    return jnp.einsum('bhqk,bkhd->bqhd', p, v)


def fox_prompt(q, k, v, F):
    B, L, H, Dh = q.shape
    nb = -(-L // Q_BLOCK)
    pad = nb * Q_BLOCK - L
    qb = jnp.pad(q, ((0, 0), (0, pad), (0, 0), (0, 0))).reshape(B, nb, Q_BLOCK, H, Dh).swapaxes(0, 1)
    Fb = jnp.pad(F, ((0, 0), (0, pad), (0, 0))).reshape(B, nb, Q_BLOCK, H).swapaxes(0, 1)
    starts = jnp.arange(nb) * Q_BLOCK

    def one(args):
        qi, Fi, s0 = args
        return fox_block(qi, Fi, s0 + jnp.arange(Q_BLOCK), k, v, F)

    o = lax.map(one, (qb, Fb, starts))
    return o.swapaxes(0, 1).reshape(B, nb * Q_BLOCK, H, Dh)[:, :L]


def mlstm_chunk(state, inp):
    C, n, m = state
    q, k, v, ig, lf = inp
    T = q.shape[1]
    b = jnp.cumsum(lf, axis=1)
    D = b[:, :, None, :] - b[:, None, :, :] + ig[:, None, :, :]
    tril = jnp.tril(jnp.ones((T, T), dtype=bool))
    D = jnp.where(tril[None, :, :, None], D, -jnp.inf)
    m_inter = b + m[:, None, :]
    m_t = jnp.maximum(m_inter, jnp.max(D, axis=2))
    w_intra = jnp.exp(D - m_t[:, :, None, :])
    w_inter = jnp.exp(m_inter - m_t)
    A = w_intra * jnp.einsum('bthd,bshd->btsh', q, k)
    num = jnp.einsum('btsh,bshd->bthd', A, v) + w_inter[..., None] * jnp.einsum('bthk,bhkv->bthv', q, C)
    den = jnp.sum(A, axis=2) + w_inter * jnp.einsum('bthk,bhk->bth', q, n)
    h = num / jnp.maximum(jnp.abs(den), jnp.exp(-m_t))[..., None]
    bL = b[:, -1]
    g = bL[:, None, :] - b + ig
    m_new = jnp.maximum(bL + m, jnp.max(g, axis=1))
    wk = jnp.exp(g - m_new[:, None, :])
    decay = jnp.exp(bL + m - m_new)
    C_new = decay[..., None, None] * C + jnp.einsum('bth,bthk,bthv->bhkv', wk, k, v)
    n_new = decay[..., None] * n + jnp.einsum('bth,bthk->bhk', wk, k)
    return (C_new, n_new, m_new), h


def mlstm_prompt(q, k, v, ig, lf):
    B, L = q.shape[:2]
    pad = (-L) % CHUNK

    def padt(a, val):
        return jnp.pad(a, ((0, 0), (pad, 0)) + ((0, 0),) * (a.ndim - 2), constant_values=val)

    q, k, v, lf = padt(q, 0.0), padt(k, 0.0), padt(v, 0.0), padt(lf, 0.0)
    ig = padt(ig, PAD_LOG_GATE)
    nc = (L + pad) // CHUNK

    def to_chunks(a):
        return a.reshape((B, nc, CHUNK) + a.shape[2:]).swapaxes(0, 1)

    init = (jnp.zeros((B, ML_HEADS, ML_HEAD_DIM, ML_HEAD_DIM), jnp.float32),
            jnp.zeros((B, ML_HEADS, ML_HEAD_DIM), jnp.float32),
            jnp.zeros((B, ML_HEADS), jnp.float32))
    state, h = lax.scan(mlstm_chunk, init, (to_chunks(q), to_chunks(k), to_chunks(v), to_chunks(ig), to_chunks(lf)))
    h = h.swapaxes(0, 1).reshape(B, L + pad, ML_HEADS, ML_HEAD_DIM)[:, pad:]
    return state, h


def token_mixers(h, p, cache):
    B, T, _ = h.shape
    f32 = jnp.float32
    z = h @ p['w_in'] + p['b_in']
    fq, fk, fv, ff, mx, mv, mo, mi, mf, ga, gb = jnp.split(z, SPLIT_IDX, axis=-1)
    fq = rmsnorm(fq.reshape(B, T, FOX_HEADS, FOX_HEAD_DIM), p['g_fq'])
    fk = rmsnorm(fk.reshape(B, T, FOX_HEADS, FOX_HEAD_DIM), p['g_fk'])
    fv = fv.reshape(B, T, FOX_HEADS, FOX_HEAD_DIM)
    logf = jax.nn.log_sigmoid(ff.astype(f32))
    if cache is None:
        F = jnp.cumsum(logf, axis=1)
        a = fox_prompt(fq, fk, fv, F)
        conv_prev = jnp.zeros((B, CONV_WIDTH - 1, ML_WIDTH), h.dtype)
    else:
        P = cache['fox_k'].shape[1]
        k_all = jnp.concatenate([cache['fox_k'].astype(fk.dtype), fk], axis=1)
        v_all = jnp.concatenate([cache['fox_v'].astype(fv.dtype), fv], axis=1)
        F = jnp.cumsum(jnp.concatenate([cache['fox_logf'].astype(f32), logf], axis=1), axis=1)
        a = fox_block(fq, F[:, P:], P + jnp.arange(T), k_all, v_all, F)
        conv_prev = cache['conv']
    xc, conv_new = causal_conv(mx, conv_prev, p['w_conv'], p['b_conv'])
    xc = jax.nn.silu(xc)
    xch = xc.reshape(B, T, ML_HEADS, ML_HEAD_DIM)
    mq = jnp.einsum('bthd,hde->bthe', xch, p['w_mq']).astype(f32)
    mk = (jnp.einsum('bthd,hde->bthe', xch, p['w_mk']) * (ML_HEAD_DIM ** -0.5)).astype(f32)
    mvh = mv.reshape(B, T, ML_HEADS, ML_HEAD_DIM).astype(f32)
    ig = mi.astype(f32)
    lfm = jax.nn.log_sigmoid(mf.astype(f32))
    if cache is None:
        (C, n, m), hcell = mlstm_prompt(mq, mk, mvh, ig, lfm)
    else:
        st = (cache['C'].astype(f32), cache['n'].astype(f32), cache['m'].astype(f32))
        (C, n, m), hcell = mlstm_chunk(st, (mq, mk, mvh, ig, lfm))
    og = jax.nn.sigmoid(mo.astype(f32)).reshape(B, T, ML_HEADS, ML_HEAD_DIM)
    hm = rmsnorm(og * hcell, p['g_mnorm']).reshape(B, T, ML_WIDTH).astype(h.dtype) + p['skip_m'] * xc
    ya = a.reshape(B, T, FOX_WIDTH) @ p['w_fox_proj']
    yb = hm @ p['w_ml_proj']
    merged = jax.nn.sigmoid(ga) * ya + jax.nn.sigmoid(gb) * yb
    out = merged @ p['w_out']
    return out, (fk, fv, logf, C, n, m, conv_new)


def trunk_layer(x, p, cache):
    h = rmsnorm(x, p['g_pre_mix'])
    mix, st = token_mixers(h, p, cache)
    x = x + rmsnorm(mix, p['g_post_mix'])
    h = rmsnorm(x, p['g_pre_ffn'])
    g, u = jnp.split(h @ p['w_gate_up'], 2, axis=-1)
    f = (jax.nn.silu(g) * u) @ p['w_down']
    x = x + rmsnorm(f, p['g_post_ffn'])
    return x, st


def stack_layers(per_layer):
    return [jnp.stack(arrs) for arrs in zip(*per_layer)]


def setup_inputs(seed: int = 0) -> dict:
    key = jax.random.key(seed)
    ks = iter(jax.random.split(key, 48))

    def nrm(shape, s=1.0):
        return s * jax.random.normal(next(ks), shape, jnp.float32)

    b_in = jnp.concatenate([
        nrm((DEPTH, 3 * FOX_WIDTH), 0.02),
        2.0 + nrm((DEPTH, FOX_HEADS), 0.5),
        nrm((DEPTH, 3 * ML_WIDTH), 0.02),
        nrm((DEPTH, ML_HEADS), 0.1),
        jnp.linspace(3.0, 6.0, ML_HEADS)[None, :] + nrm((DEPTH, ML_HEADS), 0.1),
        nrm((DEPTH, 2 * D_MODEL), 0.02)], axis=-1)
    return {
        'x_prompt': nrm((BATCH, SEQ, D_MODEL)),
        'x_sample': nrm((DEC_BATCH, DEC_SEQ, D_MODEL)),
        'cache_fox_k': nrm((DEPTH, DEC_BATCH, PAST_LEN, FOX_HEADS, FOX_HEAD_DIM)),
        'cache_fox_v': nrm((DEPTH, DEC_BATCH, PAST_LEN, FOX_HEADS, FOX_HEAD_DIM)),
        'cache_fox_logf': jax.nn.log_sigmoid(2.0 + nrm((DEPTH, DEC_BATCH, PAST_LEN, FOX_HEADS), 0.5)),
        'state_mlstm_C': nrm((DEPTH, DEC_BATCH, ML_HEADS, ML_HEAD_DIM, ML_HEAD_DIM), 0.1),
        'state_mlstm_n': nrm((DEPTH, DEC_BATCH, ML_HEADS, ML_HEAD_DIM), 0.5),
        'state_mlstm_m': nrm((DEPTH, DEC_BATCH, ML_HEADS)),
        'state_mlstm_conv': nrm((DEPTH, DEC_BATCH, CONV_WIDTH - 1, ML_WIDTH)),
        'meta_tokens': nrm((N_META, D_MODEL)),
        'w_in': nrm((DEPTH, D_MODEL, D_IN), D_MODEL ** -0.5),
        'b_in': b_in,
        'g_fq': 1.0 + nrm((DEPTH, FOX_HEADS, FOX_HEAD_DIM), 0.05),
        'g_fk': 1.0 + nrm((DEPTH, FOX_HEADS, FOX_HEAD_DIM), 0.05),
        'w_conv': nrm((DEPTH, CONV_WIDTH, ML_WIDTH), CONV_WIDTH ** -0.5),
        'b_conv': nrm((DEPTH, ML_WIDTH), 0.02),
        'w_mq': nrm((DEPTH, ML_HEADS, ML_HEAD_DIM, ML_HEAD_DIM), ML_HEAD_DIM ** -0.5),
        'w_mk': nrm((DEPTH, ML_HEADS, ML_HEAD_DIM, ML_HEAD_DIM), ML_HEAD_DIM ** -0.5),
        'g_mnorm': 1.0 + nrm((DEPTH, ML_HEADS, ML_HEAD_DIM), 0.05),
        'skip_m': 1.0 + nrm((DEPTH, ML_WIDTH), 0.05),
        'w_fox_proj': nrm((DEPTH, FOX_WIDTH, D_MODEL), FOX_WIDTH ** -0.5),
        'w_ml_proj': nrm((DEPTH, ML_WIDTH, D_MODEL), ML_WIDTH ** -0.5),
        'w_out': nrm((DEPTH, D_MODEL, D_MODEL), D_MODEL ** -0.5),
        'g_pre_mix': 1.0 + nrm((DEPTH, D_MODEL), 0.05),
        'g_post_mix': 1.0 + nrm((DEPTH, D_MODEL), 0.05),
        'g_pre_ffn': 1.0 + nrm((DEPTH, D_MODEL), 0.05),
        'g_post_ffn': 1.0 + nrm((DEPTH, D_MODEL), 0.05),
        'w_gate_up': nrm((DEPTH, D_MODEL, 2 * D_FF), D_MODEL ** -0.5),
        'w_down': nrm((DEPTH, D_FF, D_MODEL), D_FF ** -0.5),
    }


def reference(x_prompt, x_sample, cache_fox_k, cache_fox_v, cache_fox_logf, state_mlstm_C, state_mlstm_n,
              state_mlstm_m, state_mlstm_conv, meta_tokens, w_in, b_in, g_fq, g_fk, w_conv, b_conv, w_mq, w_mk,
              g_mnorm, skip_m, w_fox_proj, w_ml_proj, w_out, g_pre_mix, g_post_mix, g_pre_ffn, g_post_ffn,
              w_gate_up, w_down):
    B = x_prompt.shape[0]
    meta = jnp.broadcast_to(meta_tokens.astype(x_prompt.dtype)[None], (B, N_META, meta_tokens.shape[-1]))
    xp = jnp.concatenate([meta, x_prompt], axis=1)
    xs = x_sample
    prompt_states, sample_states = [], []
    for l in range(DEPTH):
        p = dict(w_in=w_in[l], b_in=b_in[l], g_fq=g_fq[l], g_fk=g_fk[l], w_conv=w_conv[l], b_conv=b_conv[l],
                 w_mq=w_mq[l], w_mk=w_mk[l], g_mnorm=g_mnorm[l], skip_m=skip_m[l], w_fox_proj=w_fox_proj[l],
                 w_ml_proj=w_ml_proj[l], w_out=w_out[l], g_pre_mix=g_pre_mix[l], g_post_mix=g_post_mix[l],
                 g_pre_ffn=g_pre_ffn[l], g_post_ffn=g_post_ffn[l], w_gate_up=w_gate_up[l], w_down=w_down[l])
        xp, stp = trunk_layer(xp, p, None)
        cache = dict(fox_k=cache_fox_k[l], fox_v=cache_fox_v[l], fox_logf=cache_fox_logf[l],
                     C=state_mlstm_C[l], n=state_mlstm_n[l], m=state_mlstm_m[l], conv=state_mlstm_conv[l])
        xs, sts = trunk_layer(xs, p, cache)
        prompt_states.append(stp)
        sample_states.append(sts)
    fox_k_p, fox_v_p, fox_logf_p, mlstm_C_p, mlstm_n_p, mlstm_m_p, mlstm_conv_p = stack_layers(prompt_states)
    fox_k_s, fox_v_s, fox_logf_s, mlstm_C_s, mlstm_n_s, mlstm_m_s, mlstm_conv_s = stack_layers(sample_states)
    y_prompt = xp[:, N_META:]
    y_sample = xs
    return (y_prompt, y_sample, fox_k_p, fox_v_p, fox_logf_p, mlstm_C_p, mlstm_n_p, mlstm_m_p, mlstm_conv_p,
            fox_k_s, fox_v_s, fox_logf_s, mlstm_C_s, mlstm_n_s, mlstm_m_s, mlstm_conv_s)
```

```python
import numpy as np
from contextlib import ExitStack
import concourse.bass as bass
import concourse.mybir as mybir
from concourse.bass_utils import run_bass_kernel_spmd

F32, BF16 = mybir.dt.float32, mybir.dt.bfloat16
AF = mybir.ActivationFunctionType
ALU = mybir.AluOpType
AX = mybir.AxisListType

D = 1024
SEQ = 4096
NMETA = 16
DEC = 64
PAST = 1024
DFF = 2816
EPS = 1e-6
NEG = -30000.0
SEM_LIMIT = 30000
SEQ_CD = False


class StopBuild(Exception):
    pass


class Tok:
    __slots__ = ("w", "r", "d", "name", "excl")

    def __init__(self, name="", excl=False):
        self.excl = excl
        self.w = {}
        self.r = {}
        self.d = {}
        self.name = name


class Eng:
    def __init__(self, kb, name, handle):
        self.kb, self.name, self.h = kb, name, handle
        self.sem = kb.sem(name + "_s0")
        self.cnt = 0
        self.epoch = 0
        self.seen = {}

    def wait(self, sem, val):
        if self.seen.get(sem, 0) >= val:
            return
        self.h.wait_ge(sem, val)
        self.seen[sem] = val

    def bump(self):
        if self.cnt >= SEM_LIMIT:
            self.epoch += 1
            self.sem = self.kb.sem("%s_s%d" % (self.name, self.epoch))
            self.cnt = 0
        self.cnt += 1
        return (self.sem, self.cnt)


class KB:
    def __init__(self, nc, es):
        self.nc, self.es = nc, es
        self.nsem = 0
        self.E = {}
        for name, h in (("pe", nc.tensor), ("act", nc.scalar), ("dve", nc.vector),
                        ("pool", nc.gpsimd), ("sp", nc.sync)):
            self.E[name] = Eng(self, name, h)
        self.dma_toks = []
        self.ninst = 0
        self.stop_at = 0

    def sem(self, name):
        self.nsem += 1
        return self.es.enter_context(self.nc.semaphore(name))

    def sb(self, name, shape, dt):
        return self.es.enter_context(self.nc.sbuf_tensor(name, list(shape), dt))

    def ps(self, name, shape, dt):
        return self.es.enter_context(self.nc.psum_tensor(name, list(shape), dt))

    def _deps(self, e, rd, wr):
        own = e.name == "pe"
        for b in rd:
            for s_, v in b.w.items():
                if not (own and s_ is e.sem):
                    e.wait(s_, v)
            if b.excl:
                for s_, v in b.r.items():
                    if s_ is not e.sem:
                        e.wait(s_, v)
        for b in wr:
            for s_, v in b.w.items():
                if not (own and s_ is e.sem):
                    e.wait(s_, v)
            for s_, v in b.r.items():
                if own and s_ is e.sem:
                    continue
                e.wait(s_, v)

    def prewait(self, eng, rd=(), wr=()):
        self._deps(self.E[eng], rd, wr)

    def op(self, eng, fn, rd=(), wr=()):
        if self.stop_at and self.ninst >= self.stop_at:
            raise StopBuild()
        e = self.E[eng]
        self._deps(e, rd, wr)
        ins = fn(e.h)
        tag = e.bump()
        ins.then_inc(tag[0], 1)
        self.ninst += 1
        for b in rd:
            if b.r.get(tag[0], 0) < tag[1]:
                b.r[tag[0]] = tag[1]
        for b in wr:
            b.w = {tag[0]: tag[1]}
            b.r = {}

    def dma(self, q, out, in_, tok, load, rd=(), wr=()):
        e = self.E[q]
        if load:
            self._deps(e, rd, (tok,) + tuple(wr))
        else:
            self._deps(e, (tok,) + tuple(rd), wr)
        if q not in tok.d:
            tok.d[q] = [self.sem("d%s_%s" % (q, tok.name)), 0]
            if tok not in self.dma_toks:
                self.dma_toks.append(tok)
        ds = tok.d[q]
        ds[1] += 16
        e.h.dma_start(out=out, in_=in_).then_inc(ds[0], 16)
        self.ninst += 1
        if load:
            tok.w[ds[0]] = ds[1]
            tok.r = {}
            for b in wr:
                b.w[ds[0]] = ds[1]
                b.r = {}
        else:
            tok.r[ds[0]] = ds[1]
        for b in rd:
            b.r[ds[0]] = ds[1]

    def finish(self):
        e = self.E["sp"]
        for t in self.dma_toks:
            for ds in t.d.values():
                e.wait(ds[0], ds[1])
        for name, o in self.E.items():
            if name != "sp" and o.cnt > 0:
                e.wait(o.sem, o.cnt)


class Tile:
    pass


def build(npt=32, stop=None):
    assert npt % 4 == 0
    LP = NMETA + npt * 128
    NVT = max(npt + 1, 10)
    nc = bass.Bass("TRN2", target_bir_lowering=False)
    es = ExitStack()
    kb = KB(nc, es)

    def din(name, shape):
        return nc.dram_tensor(name, list(shape), F32, kind="ExternalInput").ap()

    def dout(name, shape):
        return nc.dram_tensor(name, list(shape), F32, kind="ExternalOutput").ap()

    xp = din("xp", [npt * 128, D]); xs = din("xs", [DEC, D]); meta = din("meta", [NMETA, D])
    ck = din("ck", [PAST, 512]); cv = din("cv", [PAST, 512]); clf = din("clf", [PAST, 8])
    sC = din("sC", [4, 128, 128]); snT = din("snT", [128, 4]); sm = din("sm", [4, 1])
    sconvT = din("sconvT", [128, 4, 3])
    w_in = din("w_in", [D, 5136]); b_in = din("b_in", [5136])
    w_g16 = din("w_g16", [D, 16]); gbias = din("gbias", [8, 3]); cvec = din("cvec", [128, 68])
    g_fk = din("g_fk", [512]); g_pm = din("g_pm", [D]); g_pf = din("g_pf", [D])
    w_mq = din("w_mq", [4, 128, 128]); w_mk = din("w_mk", [4, 128, 128])
    w_fp = din("w_fp", [512, D]); w_mp = din("w_mp", [512, D]); w_out = din("w_out", [D, D])
    w_gu = din("w_gu", [D, 2 * DFF]); w_dn = din("w_dn", [DFF, D])

    o_yp = dout("o_yp", [npt * 128, D]); o_ys = dout("o_ys", [DEC, D])
    o_kp = dout("o_kp", [LP, 512]); o_vp = dout("o_vp", [LP, 512]); o_lfp = dout("o_lfp", [LP, 8])
    o_Cp = dout("o_Cp", [4, 128, 128]); o_np = dout("o_np", [128, 4, 1]); o_mp = dout("o_mp", [4, 1])
    o_cvp = dout("o_cvp", [128, 4, 3])
    o_ks = dout("o_ks", [DEC, 512]); o_vs = dout("o_vs", [DEC, 512]); o_lfs = dout("o_lfs", [DEC, 8])
    o_Cs = dout("o_Cs", [4, 128, 128]); o_ns = dout("o_ns", [128, 4, 1]); o_ms = dout("o_ms", [4, 1])
    o_cvs = dout("o_cvs", [128, 4, 3])

    sb, ps = kb.sb, kb.ps
    NK = NMETA + (NVT - 1) * 128
    KT = sb("KT", [128, 4, NK], BF16)
    VA = sb("VA", [128, NVT + 1, 8, 65], BF16)
    VAf = VA[:, :, :, :].rearrange("p t h c -> p (t h c)")
    FN = sb("FN", [128, NVT, 8], F32)
    KVt = [Tok("kv%d" % i) for i in range(NVT)]

    identb = sb("identb", [128, 128], BF16); identf = sb("identf", [128, 128], F32)
    UT = sb("UT", [128, 128], F32); MASKN = sb("MASKN", [128, 128], BF16)
    ONESF = sb("ONESF", [128, 512], F32)
    sel8 = sb("sel8", [128, 8, 128], BF16); sel4 = sb("sel4", [4, 4, 128], F32)
    CV = sb("CVEC", [128, 68], F32); GBI = sb("GBI", [8, 3], F32); NGB = sb("NGB", [8, 3], F32)
    GFQ = sb("GFQ", [128, 4], F32)
    WQ = sb("WQ", [128, 4, 128], BF16); WK = sb("WK", [128, 4, 128], BF16)
    WG = sb("WG", [128, 8, 16], BF16)
    tconst = Tok("const")
    C_GPM, C_GPF, C_GFQ, C_BFE, C_GMN, C_SKP, C_BCV, C_WCV = 0, 8, 16, 20, 40, 44, 48, 52

    NWS = 3
    WS = [sb("WS%d" % i, [128, 8, 512], BF16) for i in range(NWS)]
    WSt = [Tok("ws%d" % i) for i in range(NWS)]
    BS = [sb("BS%d" % i, [128, 512], F32) for i in range(2)]
    BSt = [Tok("bs%d" % i) for i in range(2)]
    XT = [sb("XT%d" % i, [128, D], F32) for i in range(4)]
    XTt = [Tok("xt%d" % i) for i in range(4)]
    hT = sb("hT", [128, 8, 512], BF16); hTt = Tok("hT")
    U1 = sb("U1", [128, 11264], BF16); U1t = Tok("U1")
    QTt = Tok("QTq")
    hidT = U1[:, 0:11264].rearrange("p (f n) -> p f n", f=22)
    QT = U1[:, 0:2048].rearrange("p (j n) -> p j n", j=4)
    FTq = U1[0:8, 2048:2560]
    FTq2 = U1[64:72, 2048:2560]
    FTqP = U1[0:64, 2048:2560]
    FTq2P = U1[64:128, 2048:2560]
    MXW = 518
    mxT = U1[:, 2560:2560 + 2 * 4 * MXW].bitcast(F32).rearrange("p (j n) -> p j n", j=4)
    o = 2560 + 2 * 4 * MXW
    Vm = U1[:, o:o + 4 * 4 * 129].rearrange("p (t h e) -> p t h e", t=4, h=4)
    o += 4 * 4 * 129
    OG = U1[:, o:o + 4 * 512].rearrange("p (t c) -> p t c", t=4)
    o += 4 * 512
    assert o <= 11264
    xcT = sb("xcT", [128, 4, 512], BF16); xcTt = Tok("xcT")
    aT = sb("aT", [128, 4, 512], BF16); aTt = Tok("aT")
    hmT = sb("hmT", [128, 4, 512], BF16); hmTt = Tok("hmT")
    mgT = sb("mgT", [128, 8, 512], BF16); mgTt = Tok("mgT")
    STG = [mgT[:, 0:4, :].rearrange("p k n -> p (k n)").bitcast(F32),
           mgT[:, 4:8, :].rearrange("p k n -> p (k n)").bitcast(F32),
           aT[:, :, :].rearrange("p k n -> p (k n)").bitcast(F32),
           hmT[:, :, :].rearrange("p k n -> p (k n)").bitcast(F32)]
    STGt = [mgTt, mgTt, aTt, hmTt]
    staged = {}
    NSC = 4
    SC = [sb("SC%d" % i, [128, 512], F32) for i in range(NSC)]
    SCt = [Tok("sc%d" % i) for i in range(NSC)]
    sc_i = [0]

    def sc():
        i = sc_i[0] % NSC
        sc_i[0] += 1
        return SC[i], SCt[i]

    XNs = [sb("XN%d" % i, [128, D], BF16) for i in range(2)]; XNts = [Tok("XN%d" % i) for i in range(2)]
    PTB = [sb("PTB%d" % i, [128, 512], BF16) for i in range(4)]
    PTBt = [Tok("ptb%d" % i) for i in range(4)]
    GR = [sb("GR%d" % i, [128, 512], F32) for i in range(4)]
    GRt = Tok("GR")
    GC = sb("GC", [128, 4, 16], F32); GCt = Tok("GC")
    DEC_ = sb("DECr", [4, 8], F32); DECB = sb("DECB", [128, 4, 4], F32); DECt = Tok("DEC")
    SM = sb("SM", [128, 4, 32], F32); SMts = [Tok("SM%d" % i) for i in range(4)]
    QTs = sb("QTs", [128, 128], BF16); KTs = sb("KTs", [128, 128], BF16); KKs = sb("KKs", [128, 128], BF16)
    WTs = sb("WTs", [128, 128], F32); WMs = WTs; ATs = sb("ATs", [128, 128], BF16)
    VTs = sb("VTs", [128, 132], BF16); INs = sb("INs", [128, 132], F32); NUMs = sb("NUMs", [128, 132], F32)
    HH = sb("HH", [128, 512], F32)
    mlt = {n: Tok(n) for n in ("QTs", "KTs", "KKs", "WTs", "WMs", "ATs", "VTs", "INs", "NUMs", "HH")}

    PS = [ps("PS%d" % i, [128, 512], F32) for i in range(8)]
    PSt = [Tok("ps%d" % i, excl=True) for i in range(8)]
    PA, PAt = PS[0:4], PSt[0:4]
    PM, PMt = PS[4:6], PSt[4:6]
    PSB = [PS[i][:, :].bitcast(BF16).rearrange("p (k t) -> p k t", k=8) for i in range(8)]
    TRP = [PSB[6], PSB[7]]
    TRPt = PSt[6:8]
    trp_i = [0]

    class SeqS:
        pass
    seqs = {}
    for nm in ("p", "s"):
        s = SeqS()
        s.name = nm
        s.C = sb("C_" + nm, [128, 4, 129], F32); s.Cb = sb("Cb_" + nm, [128, 4, 129], BF16)
        s.Ct = Tok("C" + nm); s.Cbt = Tok("Cb" + nm)
        s.car = sb("car_" + nm, [128, 8], F32)
        s.cart = Tok("car" + nm)
        s.HALO = sb("HALO_" + nm, [128, 4, 3], F32); s.halot = Tok("halo" + nm)
        seqs[nm] = s

    op = kb.op

    def act(out, in_, func, rd, wr, **kw):
        op("act", lambda e: e.activation(out=out, in_=in_, func=func, **kw), rd, wr)

    def tt(out, in0, in1, o_, rd, wr, eng="dve"):
        op(eng, lambda e: e.tensor_tensor(out=out, in0=in0, in1=in1, op=o_), rd, wr)

    def ts(out, in0, s1, s2, o0, o1, rd, wr, eng="dve"):
        if s2 is None:
            op(eng, lambda e: e.tensor_scalar(out=out, in0=in0, scalar1=s1, scalar2=None, op0=o0), rd, wr)
        else:
            op(eng, lambda e: e.tensor_scalar(out=out, in0=in0, scalar1=s1, scalar2=s2, op0=o0, op1=o1), rd, wr)

    def stt(out, in0, scalar, in1, o0, o1, rd, wr, eng="dve"):
        op(eng, lambda e: e.scalar_tensor_tensor(out=out, in0=in0, scalar=scalar, in1=in1, op0=o0, op1=o1), rd, wr)

    def cp(out, in_, rd, wr, eng="dve"):
        op(eng, lambda e: e.tensor_copy(out=out, in_=in_), rd, wr)

    def mm(out, lhsT, rhs, start, stop, rd, wr, skip=False):
        if skip:
            op("pe", lambda e: e.matmul(out, lhsT, rhs, start=start, stop=stop, skip_group_check=True), rd, wr)
        else:
            op("pe", lambda e: e.matmul(out, lhsT, rhs, start=start, stop=stop), rd, wr)

    def tr(out, in_, ident, rd, wr):
        op("pe", lambda e: e.transpose(out=out, in_=in_, identity=ident), rd, wr)

    def memset(ap, val, wr, eng="pool"):
        op(eng, lambda e: e.memset(ap, val), (), wr)

    def asel(out, pattern, cmp_, fill, base, cm, wr):
        op("pool", lambda e: e.affine_select(out=out, in_=out, pattern=pattern, compare_op=cmp_,
                                             fill=fill, base=base, channel_multiplier=cm), wr, wr)

    def rstd_from_ss(dst, ss, n, rd, wr):
        ts(dst, ss, 1.0 / n, EPS, ALU.mult, ALU.add, rd, wr)
        act(dst, dst, AF.Ln, wr, wr)
        act(dst, dst, AF.Exp, wr, wr, scale=-0.5)

    warmt = Tok("warm")

    def warm_ln():
        act(SM[0:1, 3, 31:32], ONESF[0:1, 0:1], AF.Ln, [tconst], [warmt])

    ws_i = [0]

    WSRC = {"w_in": w_in, "w_fp": w_fp, "w_mp": w_mp, "w_out": w_out, "w_gu": w_gu, "w_dn": w_dn}
    conv = {}

    def conv_get(name, r0, nrows, c0, ncol):
        key = (name, r0, nrows, c0, ncol)
        if key not in conv:
            scr = nc.dram_tensor("scr_%s_%d_%d_%d" % (name, r0, c0, ncol), [nrows, ncol], BF16).ap()
            tk = Tok("cv%d" % len(conv))
            kb.dma("pool", scr, WSRC[name][r0:r0 + nrows, c0:c0 + ncol], tk, True)
            conv[key] = (scr, tk)
        return conv[key]

    cur_g = [0]

    def wpiece(parts, direct=False):
        i = ws_i[0] % NWS
        ws_i[0] += 1
        for (k0, nk, c0, ncol, (name, r0, sc0)) in parts:
            scr, tk = conv_get(name, r0, nk * 128, sc0, ncol)
            if direct and cur_g[0] == 0:
                kb.dma("pool", WS[i][:, k0:k0 + nk, c0:c0 + ncol],
                       WSRC[name][r0:r0 + nk * 128, sc0:sc0 + ncol].rearrange("(k p) c -> p k c", p=128), WSt[i], True)
                continue
            kb.dma("pool", WS[i][:, k0:k0 + nk, c0:c0 + ncol],
                   scr.rearrange("(k p) c -> p k c", p=128), WSt[i], True, rd=[tk])
        return i

    def pc_B():
        for col0 in (0, 512, 1024, 2056, 2568, 1544):
            conv_get("w_in", 0, 1024, col0, 512)

    def pc_E():
        for cb in range(4):
            conv_get("w_in", 0, 1024, 3088 + cb * 256, 256)
            conv_get("w_in", 0, 1024, 4112 + cb * 256, 256)
            conv_get("w_fp", 0, 512, cb * 256, 256)
            conv_get("w_mp", 0, 512, cb * 256, 256)

    def pc_F():
        for c_ in range(2):
            conv_get("w_out", 0, 1024, c_ * 512, 512)

    def pc_Gu():
        for st in range(11):
            conv_get("w_gu", 0, 1024, st * 256, 256)
            conv_get("w_gu", 0, 1024, DFF + st * 256, 256)

    def pc_Gd():
        for c_ in range(2):
            conv_get("w_dn", 0, 1024, c_ * 512, 512)
            conv_get("w_dn", 1024, 1024, c_ * 512, 512)
            conv_get("w_dn", 2048, 768, c_ * 512, 512)

    bs_i = [0]

    def bpiece(src_vec):
        i = bs_i[0] % 2
        bs_i[0] += 1
        n = src_vec.shape[0]
        kb.dma("sp", BS[i][:, 0:n], src_vec.partition_broadcast(128), BSt[i], True)
        return i

    memset(identf[:], 1.0, [tconst]); asel(identf[:], [[-1, 128]], ALU.is_equal, 0.0, 0, 1, [tconst])
    cp(identb[:], identf[:], [tconst], [tconst], eng="pool")
    memset(UT[:], 1.0, [tconst]); asel(UT[:], [[1, 128]], ALU.is_ge, 0.0, 0, -1, [tconst])
    memset(MASKN[:], NEG, [tconst]); asel(MASKN[:], [[-1, 128]], ALU.is_gt, 0.0, 0, 1, [tconst])
    memset(ONESF[:], 1.0, [tconst])
    memset(sel8[:], 0.0, [tconst])
    memset(sel8[0:8], 1.0, [tconst]); asel(sel8[0:8], [[-1, 8], [0, 128]], ALU.is_equal, 0.0, 0, 1, [tconst])
    cp(sel8[64:72], sel8[0:8], [tconst], [tconst])
    memset(sel4[:], 1.0, [tconst]); asel(sel4[:], [[-1, 4], [0, 128]], ALU.is_equal, 0.0, 0, 1, [tconst])
    memset(VA[:, :, :, :], 0.0, KVt)
    memset(VA[:, :, :, 64:65], 1.0, KVt)
    memset(Vm[:, :, :, 128:129], 1.0, [U1t])
    kb.dma("sp", CV[:], cvec, tconst, True)
    kb.dma("sp", GBI[:], gbias, tconst, True)
    kb.dma("pool", WQ[:], w_mq.rearrange("h d e -> d h e"), tconst, True)
    kb.dma("pool", WK[:], w_mk.rearrange("h d e -> d h e"), tconst, True)
    kb.dma("pool", WG[:], w_g16.rearrange("(k p) c -> p k c", p=128), tconst, True)
    ts(NGB[:], GBI[:], -1.0, None, ALU.mult, None, [tconst], [tconst])
    ts(GFQ[:], CV[:, C_GFQ:C_GFQ + 4], 0.125, None, ALU.mult, None, [tconst], [tconst])
    P_, S_ = seqs["p"], seqs["s"]
    memset(P_.C[:], 0.0, [P_.Ct]); memset(P_.Cb[:], 0.0, [P_.Cbt]); memset(P_.car[:], 0.0, [P_.cart])
    memset(P_.HALO[:], 0.0, [P_.halot])
    memset(S_.car[:], 0.0, [S_.cart])

    SK0 = NK - (PAST + DEC)
    SV0 = NVT - 9
    assert SK0 >= NMETA

    def kt_toks(lo, hi):
        res = []
        for i in range(NVT):
            a = 0 if i == 0 else NMETA + (i - 1) * 128
            b = NMETA if i == 0 else a + 128
            if a < hi and lo < b:
                res.append(KVt[i])
        return res

    for j in range(8):
        kb.dma("pool", VA[:, SV0 + j, :, 0:64], cv[j * 128:(j + 1) * 128, :].rearrange("p (h d) -> p h d", h=8),
               tconst, True, wr=[KVt[SV0 + j]])
    wsl = ws_i[0] % NWS
    ws_i[0] += 1
    kb.dma("pool", WS[wsl][:], ck.rearrange("(j p) c -> p j c", p=128), WSt[wsl], True)
    for j in range(8):
        b = trp_i[0] % 2
        trp_i[0] += 1
        for q in range(4):
            tr(TRP[b][:, q, :], WS[wsl][:, j, q * 128:(q + 1) * 128], identb[:], [WSt[wsl], tconst], [TRPt[b]])
        lo = SK0 + j * 128
        cp(KT[:, :, lo:lo + 128], TRP[b][:, 0:4, :], [TRPt[b]], kt_toks(lo, lo + 128))
    CL = SC[0]; CLt = SCt[0]
    kb.dma("sp", CL[:, 0:64].rearrange("p (j h) -> p j h", j=8), clf.rearrange("(j p) h -> p j h", p=128), CLt, True)
    ts(CL[:, 0:64], CL[:, 0:64], -1.0, None, ALU.mult, None, [CLt], [CLt])
    for j in range(8):
        for jj in range(j + 1):
            lhs = UT[:] if jj == j else ONESF[:, 0:128]
            mm(PM[0][:, j * 8:(j + 1) * 8], lhs, CL[:, jj * 8:(jj + 1) * 8], jj == 0, jj == j,
               [CLt, tconst], [PMt[0]], skip=True)
    cp(FN[:, SV0:SV0 + 8, :], PM[0][:, 0:64].rearrange("p (j h) -> p j h", j=8), [PMt[0]], KVt[SV0:SV0 + 8])
    for j in range(8):
        mm(PM[1][0:8, 0:1], CL[:, j * 8:(j + 1) * 8], ONESF[:, 0:1], j == 0, j == 7, [CLt, tconst], [PMt[1]])
    cp(S_.car[32:40, 0:1], PM[1][0:8, 0:1], [PMt[1]], [S_.cart])
    kb.dma("sp", S_.C[:, :, 0:128], sC.rearrange("h k v -> k h v"), S_.Ct, True)
    NTs = sb("NTs", [128, 8], F32); NTst = Tok("NTs")
    kb.dma("sp", NTs[:, 0:4], snT, NTst, True)
    cp(S_.C[:, :, 128:129], NTs[:, 0:4].unsqueeze(2), [NTst], [S_.Ct])
    cp(S_.Cb[:], S_.C[:], [S_.Ct], [S_.Cbt])
    kb.dma("sp", S_.car[0:4, 2:3], sm, S_.cart, True)
    cp(S_.car[0:4, 3:4], S_.car[0:4, 2:3], [S_.cart], [S_.cart])
    kb.dma("sp", S_.HALO[:], sconvT, S_.halot, True)

    pc_B()

    def mk_tile(seq, T, idx):
        t = Tile()
        t.seq, t.T, t.idx = seqs[seq], T, idx
        if seq == "p":
            t.vi = idx
            t.k0 = 0 if idx == 0 else NMETA + (idx - 1) * 128
            t.first, t.last = idx == 0, idx == npt
        else:
            t.vi = SV0 + 8
            t.k0 = SK0 + PAST
            t.first, t.last = True, True
        t.ffn = not (seq == "p" and idx == 0)
        return t

    groups = [[mk_tile("p", NMETA, 0), mk_tile("p", 128, 1), mk_tile("p", 128, 2), mk_tile("s", DEC, 0)]]
    nxt = 3
    while nxt <= npt:
        n_ = min(4, npt - nxt + 1)
        groups.append([mk_tile("p", 128, nxt + i) for i in range(n_)])
        nxt += n_

    if isinstance(stop, int):
        kb.stop_at = stop

    kb.marks = []

    def chk(name, gi):
        kb.marks.append((gi, name, kb.E["pe"].cnt + kb.E["pe"].epoch * SEM_LIMIT))
        if stop is not None and stop == "%s%d" % (name, gi):
            raise StopBuild()

    try:
      for gi, tiles in enumerate(groups):
          chk("start", gi)
          cur_g[0] = gi
          c = 0
          mc = 0
          prev_seq = None
          for i, t in enumerate(tiles):
              t.slot = i
              t.c0 = c
              if t.seq is not prev_seq:
                  mc += 3
                  prev_seq = t.seq
              t.m0 = mc
              c += t.T
              mc += t.T
          N = c
          runs = []
          for t in tiles:
              if runs and runs[-1][0] is t.seq:
                  runs[-1][2] = t.c0 + t.T
                  runs[-1][4].append(t)
              else:
                  runs.append([t.seq, t.c0, t.c0 + t.T, t.m0, [t]])

          def rr(gens):
              gens = list(gens)
              while gens:
                  for g_ in list(gens):
                      try:
                          next(g_)
                      except StopIteration:
                          gens.remove(g_)

          def norm_to_T(src, srct, t, gcol, dstT, dstt):
              T = t.T
              XN, XNt, SMt = XNs[t.slot % 2], XNts[t.slot % 2], SMts[t.slot]
              ssc = SM[0:T, t.slot, 0:1]
              rsc = SM[0:T, t.slot, 1:2]
              act(XN[0:T, :], src[0:T, :], AF.Square, [srct], [XNt, SMt], accum_out=ssc)
              yield
              ts(rsc, ssc, 1.0 / D, EPS, ALU.mult, ALU.add, [SMt], [SMt])
              yield
              act(rsc, rsc, AF.Ln, [SMt], [SMt])
              yield
              act(rsc, rsc, AF.Exp, [SMt], [SMt], scale=-0.5)
              yield
              ts(XN[0:T, :], src[0:T, :], rsc, None, ALU.mult, None, [srct, SMt], [XNt])
              yield
              b = t.slot % 2
              for k in range(8):
                  tr(TRP[b][:, k, 0:T], XN[0:T, k * 128:(k + 1) * 128], identb[0:T, 0:T], [XNt, tconst], [TRPt[b]])
              yield
              tt(dstT[:, :, t.c0:t.c0 + T], TRP[b][:, :, 0:T],
                 CV[:, gcol:gcol + 8].unsqueeze(2).broadcast_to([128, 8, T]), ALU.mult,
                 [TRPt[b], tconst], [dstt])
              yield

          for t in tiles:
              T = t.T
              if t.seq.name == "p":
                  src = meta if t.idx == 0 else xp[(t.idx - 1) * 128:t.idx * 128, :]
              else:
                  src = xs
              if (gi, t.slot) in staged:
                  kb.dma("sp", XT[t.slot][0:T, :], STG[t.slot][0:T, :], XTt[t.slot], True, rd=[STGt[t.slot]])
              else:
                  kb.dma("sp", XT[t.slot][0:T, :], src, XTt[t.slot], True)
          for i0 in range(0, len(tiles), 2):
              rr([norm_to_T(STG[t.slot] if (gi, t.slot) in staged else XT[t.slot],
                            STGt[t.slot] if (gi, t.slot) in staged else XTt[t.slot], t, C_GPM, hT, hTt)
                  for t in tiles[i0:i0 + 2]])

          chk("A", gi)
          if gi == 0:
              pc_E()
          def grow(a, b, n):
              return GR[a][b:b + n, :]
          IG, LF2, BB, MM = grow(0, 0, 4), grow(1, 0, 4), grow(2, 0, 4), grow(3, 0, 4)
          Lf, CSf = grow(0, 32, 8), grow(1, 32, 8)
          EM, WI, WKr = grow(0, 64, 4), grow(1, 64, 4), grow(2, 64, 4)
          UU, NM = IG, MM
          for k in range(8):
              mm(PM[0][0:8, 0:N], WG[:, k, 0:8], hT[:, k, 0:N], k == 0, k == 7, [hTt, tconst], [PMt[0]])
          for k in range(8):
              mm(PM[1][0:4, 0:N], WG[:, k, 8:12], hT[:, k, 0:N], k == 0, k == 7, [hTt, tconst], [PMt[1]])
          for k in range(8):
              mm(PS[6][0:4, 0:N], WG[:, k, 12:16], hT[:, k, 0:N], k == 0, k == 7, [hTt, tconst], [PSt[6]])
          act(Lf[:, 0:N], PM[0][0:8, 0:N], AF.Exp, [PMt[0], tconst], [GRt], scale=-1.0, bias=NGB[0:8, 0:1])
          act(Lf[:, 0:N], Lf[:, 0:N], AF.Ln, [GRt], [GRt], bias=1.0)
          act(IG[:, 0:N], PM[1][0:4, 0:N], AF.Identity, [PMt[1], tconst], [GRt], bias=GBI[0:4, 1:2])
          act(LF2[:, 0:N], PS[6][0:4, 0:N], AF.Exp, [PSt[6], tconst], [GRt], scale=-1.0, bias=NGB[0:4, 2:3])
          act(LF2[:, 0:N], LF2[:, 0:N], AF.Ln, [GRt], [GRt], bias=1.0)
          ts(LF2[:, 0:N], LF2[:, 0:N], -1.0, None, ALU.mult, None, [GRt], [GRt])
          for r in runs:
              s, a, b = r[0], r[1], r[2]
              op("dve", lambda e: e.tensor_tensor_scan(out=CSf[:, a:b], data0=ONESF[32:40, 0:b - a],
                                                       data1=Lf[:, a:b], initial=s.car[32:40, 0:1],
                                                       op0=ALU.mult, op1=ALU.add), [GRt, s.cart, tconst], [GRt])
              op("dve", lambda e: e.tensor_tensor_scan(out=BB[:, a:b], data0=ONESF[0:4, 0:b - a], data1=LF2[:, a:b],
                                                       initial=s.car[0:4, 1:2], op0=ALU.mult, op1=ALU.add),
                 [GRt, s.cart, tconst], [GRt])
              op("dve", lambda e: e.tensor_tensor_scan(out=MM[:, a:b], data0=LF2[:, a:b], data1=IG[:, a:b],
                                                       initial=s.car[0:4, 2:3], op0=ALU.add, op1=ALU.max),
                 [GRt, s.cart], [GRt])
          for r in runs:
              s, b = r[0], r[2]
              cp(s.car[32:40, 0:1], CSf[:, b - 1:b], [GRt], [s.cart])
              cp(s.car[0:4, 1:2], BB[:, b - 1:b], [GRt], [s.cart])
              cp(s.car[0:4, 2:3], MM[:, b - 1:b], [GRt], [s.cart])
          tt(UU[:, 0:N], IG[:, 0:N], BB[:, 0:N], ALU.subtract, [GRt], [GRt])
          act(EM[:, 0:N], MM[:, 0:N], AF.Exp, [GRt], [GRt], scale=-1.0)
          tt(NM[:, 0:N], BB[:, 0:N], MM[:, 0:N], ALU.subtract, [GRt], [GRt])
          memset(U1[:, 2048:2560], 0.0, [QTt], eng="dve")
          ts(FTq[:, 0:N], CSf[:, 0:N], -1.0, None, ALU.mult, None, [GRt], [QTt])
          ts(FTq2[:, 0:N], CSf[:, 0:N], -1.0, None, ALU.mult, None, [GRt], [QTt])
          def rr_until(gens):
              live = list(gens)
              while live:
                  for g_ in list(live):
                      try:
                          r_ = next(g_)
                      except StopIteration:
                          live.remove(g_)
                          continue
                      if r_ == "XN":
                          live.remove(g_)

          def tokmajor(col0, evac, bb, pend=None, defer=False):
              wi = wpiece([(0, 8, 0, 512, ("w_in", 0, col0))], direct=True)
              bi = bpiece(b_in[col0:col0 + 512])
              for k in range(8):
                  for t in tiles:
                      mm(PS[bb + t.slot][0:t.T, :], hT[:, k, t.c0:t.c0 + t.T], WS[wi][:, k, :], k == 0, k == 7,
                         [hTt, WSt[wi]], [PSt[bb + t.slot]])
              if pend:
                  for i0 in range(0, len(pend), 2):
                      rr(pend[i0:i0 + 2])
              gens = [evac(t, bi, bb + t.slot) for t in tiles]
              for i0 in range(0, len(gens), 2):
                  if defer:
                      rr_until(gens[i0:i0 + 2])
                  else:
                      rr(gens[i0:i0 + 2])
              return gens if defer else None

          def headnorm_g(Z, Zt, t, nh, hd, rcol):
              T = t.T
              SMt = SMts[t.slot]
              sq, sqt = sc()
              act(sq[0:T, :], Z[0:T, :], AF.Square, [Zt], [sqt])
              yield
              op("dve", lambda e: e.tensor_reduce(out=SM[0:T, t.slot, rcol:rcol + nh],
                                                  in_=sq[0:T, :].rearrange("p (h d) -> p h d", h=nh),
                                                  axis=AX.X, op=ALU.add), [sqt], [SMt])
              dst = SM[0:T, t.slot, rcol:rcol + nh]
              ts(dst, dst, 1.0 / hd, EPS, ALU.mult, ALU.add, [SMt], [SMt])
              yield
              act(dst, dst, AF.Ln, [SMt], [SMt])
              yield
              act(dst, dst, AF.Exp, [SMt], [SMt], scale=-0.5)
              yield

          def headnorm(Z, Zt, t, nh, hd, rcol):
              for _ in headnorm_g(Z, Zt, t, nh, hd, rcol):
                  pass

          def ev_fq(t, bi, bk):
              T = t.T
              XN, XNt, SMt = XNs[t.slot % 2], XNts[t.slot % 2], SMts[t.slot]
              z, zt = sc()
              tt(z[0:T, :], PS[bk][0:T, :], BS[bi][0:T, :], ALU.add, [PSt[bk], BSt[bi]], [zt])
              yield
              yield from headnorm_g(z, zt, t, 8, 64, 2)
              xo = (t.slot // 2) * 512
              tt(XN[0:T, xo:xo + 512].rearrange("p (h d) -> p h d", h=8), z[0:T, :].rearrange("p (h d) -> p h d", h=8),
                 SM[0:T, t.slot, 2:10].unsqueeze(2).broadcast_to([T, 8, 64]), ALU.mult, [zt, SMt], [XNt])
              yield "XN"
              for q in range(4):
                  tr(PSB[bk][:, q, 0:T], XN[0:T, xo + q * 128:xo + (q + 1) * 128], identb[0:T, 0:T], [XNt, tconst], [PSt[bk]])
              yield
              tt(QT[:, :, t.c0:t.c0 + T], PSB[bk][:, 0:4, 0:T], GFQ[:].unsqueeze(2).broadcast_to([128, 4, T]),
                 ALU.mult, [PSt[bk], tconst], [QTt])
              yield

          gfk_slot = [None]

          def ev_fk(t, bi, bk):
              T = t.T
              XN, XNt, SMt = XNs[t.slot % 2], XNts[t.slot % 2], SMts[t.slot]
              if gfk_slot[0] is None:
                  gfk_slot[0] = bpiece(g_fk)
              gi_ = gfk_slot[0]
              z, zt = sc()
              tt(z[0:T, :], PS[bk][0:T, :], BS[bi][0:T, :], ALU.add, [PSt[bk], BSt[bi]], [zt])
              yield
              yield from headnorm_g(z, zt, t, 8, 64, 2)
              tt(z[0:T, :].rearrange("p (h d) -> p h d", h=8), z[0:T, :].rearrange("p (h d) -> p h d", h=8),
                 SM[0:T, t.slot, 2:10].unsqueeze(2).broadcast_to([T, 8, 64]), ALU.mult, [zt, SMt], [zt])
              yield
              tt(z[0:T, :], z[0:T, :], BS[gi_][0:T, :], ALU.mult, [zt, BSt[gi_]], [zt])
              yield
              dst = (o_kp if t.seq.name == "p" else o_ks)
              r0 = t.k0 if t.seq.name == "p" else 0
              kb.dma("sp", dst[r0:r0 + T, :], z[0:T, :], zt, False)
              xo = (t.slot // 2) * 512
              act(XN[0:T, xo:xo + 512], z[0:T, :], AF.Copy, [zt], [XNt])
              yield "XN"
              for q in range(4):
                  tr(PSB[bk][:, q, 0:T], XN[0:T, xo + q * 128:xo + (q + 1) * 128], identb[0:T, 0:T], [XNt, tconst], [PSt[bk]])
              yield
              cp(KT[:, :, t.k0:t.k0 + T], PSB[bk][:, 0:4, 0:T], [PSt[bk]], kt_toks(t.k0, t.k0 + T))
              yield

          def ev_fv(t, bi, bk):
              T = t.T
              z, zt = sc()
              tt(z[0:T, :], PS[bk][0:T, :], BS[bi][0:T, :], ALU.add, [PSt[bk], BSt[bi]], [zt])
              yield
              dst = (o_vp if t.seq.name == "p" else o_vs)
              r0 = t.k0 if t.seq.name == "p" else 0
              kb.dma("sp", dst[r0:r0 + T, :], z[0:T, :], zt, False)
              act(VA[0:T, t.vi, :, 0:64], z[0:T, :].rearrange("p (h d) -> p h d", h=8), AF.Copy, [zt], [KVt[t.vi]])
              yield

          def ev_mv(t, bi, bk):
              T = t.T
              tt(Vm[0:T, t.slot, :, 0:128], PS[bk][0:T, :].rearrange("p (h d) -> p h d", h=4),
                 BS[bi][0:T, :].rearrange("p (h d) -> p h d", h=4), ALU.add, [PSt[bk], BSt[bi]], [U1t])
              yield

          def ev_mo(t, bi, bk):
              T = t.T
              z, zt = sc()
              tt(z[0:T, :], PS[bk][0:T, :], BS[bi][0:T, :], ALU.add, [PSt[bk], BSt[bi]], [zt])
              yield
              act(OG[0:T, t.slot, :], z[0:T, :], AF.Sigmoid, [zt], [U1t])
              yield

          pend_ = tokmajor(0, ev_fq, 0, None, True)
          pend_ = tokmajor(512, ev_fk, 4, pend_, True)
          tokmajor(1024, ev_fv, 0, pend_, False)
          tokmajor(2056, ev_mv, 4)
          tokmajor(2568, ev_mo, 0)

          chk("B1", gi)
          for r in runs:
              cp(mxT[:, :, r[3] - 3:r[3]], r[0].HALO[:], [r[0].halot], [U1t])
          wi = wpiece([(0, 8, 0, 512, ("w_in", 0, 1544))], direct=True)
          for j in range(4):
              bank = j % 4
              for k in range(8):
                  mm(PA[bank][:, 0:N], WS[wi][:, k, j * 128:(j + 1) * 128], hT[:, k, 0:N], k == 0, k == 7,
                     [hTt, WSt[wi]], [PAt[bank]])
              for r in runs:
                  n = r[2] - r[1]
                  act(mxT[:, j, r[3]:r[3] + n], PA[bank][:, r[1]:r[2]], AF.Identity, [PAt[bank], tconst], [U1t],
                      bias=CV[:, C_BFE + j:C_BFE + j + 1])
              for r in runs:
                  n = r[2] - r[1]
                  m0 = r[3]
                  cvt, cvtt = sc()
                  ts(cvt[:, 0:n], mxT[:, j, m0 - 3:m0 - 3 + n], CV[:, C_WCV + 4 * j:C_WCV + 4 * j + 1],
                     CV[:, C_BCV + j:C_BCV + j + 1], ALU.mult, ALU.add, [U1t, tconst], [cvtt])
                  for i in range(1, 4):
                      stt(cvt[:, 0:n], mxT[:, j, m0 - 3 + i:m0 - 3 + i + n],
                          CV[:, C_WCV + 4 * j + i:C_WCV + 4 * j + i + 1], cvt[:, 0:n], ALU.mult, ALU.add,
                          [U1t, tconst, cvtt], [cvtt])
                  act(xcT[:, j, r[1]:r[2]], cvt[:, 0:n], AF.Silu, [cvtt], [xcTt])
          for r in runs:
              n = r[2] - r[1]
              cp(r[0].HALO[:], mxT[:, :, r[3] + n - 3:r[3] + n], [U1t], [r[0].halot])

          chk("B2", gi)
          if gi == 0:
              pc_F()
              pc_Gu()
          chk("B3", gi)
          SQK = float(128 ** -0.5)

          def gen_C():
              for t in tiles:
                  s, T, a = t.seq, t.T, t.c0
                  b = a + T
                  act(WI[:, a:b], NM[:, a:b], AF.Exp, [GRt, s.cart], [GRt], bias=s.car[0:4, 3:4])
                  act(WKr[:, a:b], UU[:, a:b], AF.Exp, [GRt], [GRt], bias=NM[:, b - 1:b])
                  act(DEC_[0:4, t.slot:t.slot + 1], NM[:, b - 1:b], AF.Exp, [GRt, s.cart], [DECt], bias=s.car[0:4, 3:4])
                  ts(s.car[0:4, 3:4], NM[:, b - 1:b], -1.0, None, ALU.mult, None, [GRt], [s.cart])
                  for qi, (src, pb) in enumerate(((UU, 0), (WI, 64), (WKr, 64), (EM, 64))):
                      tr(PS[6][0:T, qi * 4:(qi + 1) * 4], src[:, a:b], identf[pb:pb + 4, pb:pb + 4], [GRt, tconst], [PSt[6]])
                  cp(GC[0:T, t.slot, :], PS[6][0:T, 0:16], [PSt[6]], [GCt])
                  tr(PS[7][0:T, 0:8], CSf[:, a:b], identf[32:40, 32:40], [GRt, tconst], [PSt[7]])
                  tr(PS[7][0:T, 8:16], Lf[:, a:b], identf[32:40, 32:40], [GRt, tconst], [PSt[7]])
                  cp(FN[0:T, t.vi, :], PS[7][0:T, 0:8], [PSt[7]], [KVt[t.vi]])
                  lo_, lot = sc()
                  ts(lo_[0:T, 0:8], PS[7][0:T, 8:16], -1.0, None, ALU.mult, None, [PSt[7]], [lot])
                  dst = (o_lfp if s.name == "p" else o_lfs)
                  r0 = t.k0 if s.name == "p" else 0
                  kb.dma("sp", dst[r0:r0 + T, :], lo_[0:T, 0:8], lot, False)
                  yield
              nt = len(tiles)
              for h in range(4):
                  mm(PS[6][:, 32 + h * 4:32 + h * 4 + nt], sel4[:, h, :], DEC_[0:4, 0:nt], True, True,
                     [DECt, tconst], [PSt[6]])
              cp(DECB[:, :, 0:nt], PS[6][:, 32:48].rearrange("p (h t) -> p h t", h=4)[:, :, 0:nt], [PSt[6]], [DECt])
              yield

              X, Xt, Y, Yt = PS[6], PSt[6], PS[7], PSt[7]
              for t in tiles:
                  s, T, a = t.seq, t.T, t.c0
                  b = a + T
                  for h in range(4):
                      mm(X[:, 0:T], WQ[:, h, :], xcT[:, h, a:b], True, True, [xcTt, tconst], [Xt])
                      mm(X[:, 128:128 + T], WK[:, h, :], xcT[:, h, a:b], True, True, [xcTt, tconst], [Xt])
                      mm(X[0:T, 256:384], xcT[:, h, a:b], WK[:, h, :], True, True, [xcTt, tconst], [Xt])
                      mm(X[0:T, 384:384 + T], sel4[:, h, 0:T], NM[:, a:b], True, True, [GRt, tconst], [Xt])
                      cp(QTs[:, 0:T], X[:, 0:T], [Xt], [mlt["QTs"]])
                      ts(KTs[:, 0:T], X[:, 128:128 + T], SQK, None, ALU.mult, None, [Xt], [mlt["KTs"]])
                      ts(KKs[0:T, :], X[0:T, 256:384], SQK, None, ALU.mult, None, [Xt], [mlt["KKs"]])
                      act(WTs[0:T, 0:T], X[0:T, 384:384 + T], AF.Exp, [Xt, GCt], [mlt["WTs"]], bias=GC[0:T, t.slot, h:h + 1])
                      tt(WTs[0:T, 0:T], WTs[0:T, 0:T], UT[0:T, 0:T], ALU.mult, [mlt["WTs"], tconst], [mlt["WTs"]])
                      ts(VTs[0:T, 0:129], Vm[0:T, t.slot, h, :], GC[0:T, t.slot, 8 + h:9 + h], None, ALU.mult, None,
                         [U1t, GCt], [mlt["VTs"]])
                      yield
                      mm(X[0:T, 384:384 + T], KTs[:, 0:T], QTs[:, 0:T], True, True, [mlt["KTs"], mlt["QTs"]], [Xt])
                      tt(ATs[0:T, 0:T], X[0:T, 384:384 + T], WTs[0:T, 0:T], ALU.mult, [Xt, mlt["WTs"]], [mlt["ATs"]])
                      yield
                      mm(Y[0:T, 128:257], ATs[0:T, 0:T], Vm[0:T, t.slot, h, :], True, True, [mlt["ATs"], U1t], [Yt])
                      mm(Y[0:T, 257:386], QTs[:, 0:T], s.Cb[:, h, :], True, True, [mlt["QTs"], s.Cbt], [Yt])
                      mm(X[:, 0:129], KKs[0:T, :], VTs[0:T, 0:129], True, True, [mlt["KKs"], mlt["VTs"]], [Xt])
                      ts(INs[0:T, 0:129], Y[0:T, 257:386], GC[0:T, t.slot, 4 + h:5 + h], None, ALU.mult, None, [Yt, GCt], [mlt["INs"]])
                      tt(NUMs[0:T, 0:129], Y[0:T, 128:257], INs[0:T, 0:129], ALU.add, [Yt, mlt["INs"]], [mlt["NUMs"]])
                      stt(s.C[:, h, :], s.C[:, h, :], DECB[:, h, t.slot:t.slot + 1], X[:, 0:129], ALU.mult, ALU.add,
                          [s.Ct, DECt, Xt], [s.Ct])
                      cp(s.Cb[:, h, :], s.C[:, h, :], [s.Ct], [s.Cbt])
                      act(NUMs[0:T, 129:130], NUMs[0:T, 128:129], AF.Abs, [mlt["NUMs"]], [mlt["NUMs"]])
                      tt(NUMs[0:T, 130:131], NUMs[0:T, 129:130], GC[0:T, t.slot, 12 + h:13 + h], ALU.max,
                         [mlt["NUMs"], GCt], [mlt["NUMs"]])
                      op("dve", lambda e: e.reciprocal(out=NUMs[0:T, 131:132], in_=NUMs[0:T, 130:131]),
                         [mlt["NUMs"]], [mlt["NUMs"]])
                      stt(HH[0:T, h * 128:(h + 1) * 128], NUMs[0:T, 0:128], NUMs[0:T, 131:132],
                          OG[0:T, t.slot, h * 128:(h + 1) * 128], ALU.mult, ALU.mult, [mlt["NUMs"], U1t], [mlt["HH"]])
                      yield
                  XN, XNt, SMt = XNs[t.slot % 2], XNts[t.slot % 2], SMts[t.slot]
                  headnorm(HH, mlt["HH"], t, 4, 128, 10)
                  tt(XN[0:T, 0:512].rearrange("p (h d) -> p h d", h=4), HH[0:T, :].rearrange("p (h d) -> p h d", h=4),
                     SM[0:T, t.slot, 10:14].unsqueeze(2).broadcast_to([T, 4, 128]), ALU.mult, [mlt["HH"], SMt], [XNt])
                  yield
                  tmp, tmpt = sc()
                  tt(tmp[:, 0:4 * T].rearrange("p (h t) -> p h t", h=4), xcT[:, :, a:b],
                     CV[:, C_SKP:C_SKP + 4].unsqueeze(2).broadcast_to([128, 4, T]), ALU.mult, [xcTt, tconst], [tmpt])
                  bq = 1
                  for q in range(4):
                      tr(TRP[bq][:, q, 0:T], XN[0:T, q * 128:(q + 1) * 128], identb[0:T, 0:T], [XNt, tconst], [TRPt[bq]])
                  for q in range(4):
                      stt(hmT[:, q, a:b], TRP[bq][:, q, 0:T], CV[:, C_GMN + q:C_GMN + q + 1], tmp[:, q * T:(q + 1) * T],
                          ALU.mult, ALU.add, [TRPt[bq], tconst, tmpt], [hmTt])
                  if t.last:
                      pre = "p" if s.name == "p" else "s"
                      oC, on, om, ocv = ((o_Cp, o_np, o_mp, o_cvp) if pre == "p" else (o_Cs, o_ns, o_ms, o_cvs))
                      kb.dma("sp", oC.rearrange("h k v -> k h v"), s.C[:, :, 0:128], s.Ct, False)
                      ncol = 0 if pre == "p" else 4
                      cp(NTs[:, ncol:ncol + 4].unsqueeze(2), s.C[:, :, 128:129], [s.Ct], [NTst])
                      kb.dma("sp", on.rearrange("k h o -> k (h o)"), NTs[:, ncol:ncol + 4], NTst, False)
                      kb.dma("sp", om, s.car[0:4, 2:3], s.cart, False)
                      kb.dma("sp", ocv, s.HALO[:], s.halot, False)
                  yield

          def gen_D():
            for r in runs:
              s, qa, qb = r[0], r[1], r[2]
              Nq = qb - qa
              rt = r[4]
              if s.name == "p":
                  last_idx = rt[-1].idx
                  ktl = []
                  for i in range(last_idx + 1):
                      Tk = NMETA if i == 0 else 128
                      k0 = 0 if i == 0 else NMETA + (i - 1) * 128
                      own = [x for x in rt if x.idx == i]
                      qs = (own[0].c0 - qa) if own else 0
                      ktl.append((k0, Tk, i, qs, bool(own)))
              else:
                  ktl = [(SK0 + j * 128, 128, SV0 + j, 0, False) for j in range(8)]
                  ktl.append((rt[0].k0, rt[0].T, rt[0].vi, 0, True))
              blocks = []
              for jp in range(4):
                  for ki, kt_ in enumerate(ktl):
                      blocks.append((jp, ki) + kt_)
              nkt = len(ktl)
              LA = 1

              def emit_S(i):
                  jp, ki, k0, Tk, vi, qs, diag = blocks[i]
                  ktk = kt_toks(k0, k0 + Tk)
                  for r_ in range(2):
                      rr = r_ * 64
                      Sb, Sbt = PA[2 * (i % 2) + r_], PAt[2 * (i % 2) + r_]
                      mm(Sb[0:Tk, qs:Nq], KT[rr:rr + 64, jp, k0:k0 + Tk], QT[rr:rr + 64, jp, qa + qs:qb], True, False,
                         ktk + [QTt], [Sbt], skip=True)
                  for r_ in range(2):
                      h = 2 * jp + r_
                      Sb, Sbt = PA[2 * (i % 2) + r_], PAt[2 * (i % 2) + r_]
                      if r_ == 0:
                          mm(Sb[0:Tk, qs:Nq], sel8[0:64, h, 0:Tk], FTqP[:, qa + qs:qb], False, not diag,
                             [tconst, QTt], [Sbt], skip=True)
                      else:
                          mm(Sb[0:Tk, qs:Nq], sel8[64:128, h, 0:Tk], FTq2P[:, qa + qs:qb], False, not diag,
                             [tconst, QTt], [Sbt], skip=True)
                  if diag:
                      for r_ in range(2):
                          Sb, Sbt = PA[2 * (i % 2) + r_], PAt[2 * (i % 2) + r_]
                          mm(Sb[0:Tk, qs:qs + Tk], identb[0:Tk, 0:Tk], MASKN[0:Tk, 0:Tk], False, True,
                             [tconst], [Sbt], skip=True)

              def emit_PV(i):
                  jp, ki, k0, Tk, vi, qs, diag = blocks[i]
                  for r_ in range(2):
                      h = 2 * jp + r_
                      Sb, Sbt = PA[2 * (i % 2) + r_], PAt[2 * (i % 2) + r_]
                      pb_, pbt = PTB[2 * (i % 2) + r_], PTBt[2 * (i % 2) + r_]
                      act(pb_[0:Tk, qs:Nq], Sb[0:Tk, qs:Nq], AF.Exp, [Sbt, KVt[vi]], [pbt], bias=FN[0:Tk, vi, h:h + 1])
                  for r_ in range(2):
                      h = 2 * jp + r_
                      pb_, pbt = PTB[2 * (i % 2) + r_], PTBt[2 * (i % 2) + r_]
                      OB, OBt = PM[r_], PMt[r_]
                      vo = (vi * 8 + h) * 65
                      mm(OB[:, qs:Nq], VAf[0:Tk, vo:vo + 128], pb_[0:Tk, qs:Nq], ki == 0, ki == nkt - 1,
                         [KVt[vi], pbt], [OBt], skip=True)

              def emit_norm_pair(jp_, bk0):
                  rds = []
                  for r_ in range(2):
                      OB, OBt = PM[r_], PMt[r_]
                      rd_, rdt = sc()
                      act(rd_[64:65, 0:Nq], OB[64:65, 0:Nq], AF.Ln, [OBt], [rdt])
                      act(rd_[64:65, 0:Nq], rd_[64:65, 0:Nq], AF.Exp, [rdt], [rdt], scale=-1.0)
                      rds.append((rd_, rdt))
                  for r_ in range(2):
                      rd_, rdt = rds[r_]
                      bk = bk0 + r_
                      mm(PA[bk][0:64, 0:Nq], ONESF[64:65, 0:64], rd_[64:65, 0:Nq], True, True, [rdt, tconst], [PAt[bk]])
                  for r_ in range(2):
                      h = 2 * jp_ + r_
                      j, rr = h // 2, (h % 2) * 64
                      OB, OBt = PM[r_], PMt[r_]
                      bk = bk0 + r_
                      rb, rbt = rds[r_]
                      act(rb[0:64, 0:Nq], PA[bk][0:64, 0:Nq], AF.Copy, [PAt[bk]], [rbt])
                      tt(aT[rr:rr + 64, j, qa:qb], OB[0:64, 0:Nq], rb[0:64, 0:Nq], ALU.mult, [OBt, rbt], [aTt])

              nb = len(blocks)
              for i in range(nb + LA):
                  if i < nb:
                      emit_S(i)
                  jb = i - LA
                  if jb >= 0:
                      emit_PV(jb)
                      if blocks[jb][1] == nkt - 1:
                          emit_norm_pair(blocks[jb][0], 2 * (jb % 2))
                  yield

          nC = sum(13 for t in tiles) + 2 * len(tiles) + 1
          nD = 0
          for r in runs:
              nk_ = (r[4][-1].idx + 1) if r[0].name == "p" else 9
              nD += 4 * nk_ + 1
          gC, gD = gen_C(), gen_D()
          if SEQ_CD:
              for _ in gC:
                  pass
              for _ in gD:
                  pass
          npre = len(tiles) + 1
          pre_rate = npre if gi == 0 else 2
          doneC = doneD = 0
          aliveC = aliveD = True
          while aliveC or aliveD:
              if aliveD:
                  try:
                      next(gD)
                      doneD += 1
                  except StopIteration:
                      aliveD = False
              if aliveC:
                  remD = max(nD - doneD, 0)
                  if doneC < npre:
                      want = min(pre_rate, npre - doneC)
                  elif remD == 0 or not aliveD:
                      want = max(nC - doneC, 1)
                  else:
                      want = -(-(nC - doneC) // (remD + 1))
                  for _ in range(want):
                      try:
                          next(gC)
                          doneC += 1
                      except StopIteration:
                          aliveC = False
                          break

          chk("D", gi)
          if gi == 0:
              pc_Gd()
          for cb in range(4):
              c0 = cb * 256
              wa = wpiece([(0, 8, 0, 256, ("w_in", 0, 3088 + c0)), (0, 8, 256, 256, ("w_in", 0, 4112 + c0))])
              wb = wpiece([(0, 4, 0, 256, ("w_fp", 0, c0)), (4, 4, 0, 256, ("w_mp", 0, c0))])
              for jj in range(2):
                  f = cb * 2 + jj
                  cs = slice(jj * 128, (jj + 1) * 128)
                  cs2 = slice(256 + jj * 128, 256 + (jj + 1) * 128)
                  bks = (0, 1, 2, 3) if f % 2 == 0 else (4, 5, 6, 7)
                  Ea, Eat, Eb, Ebt = PS[bks[0]], PSt[bks[0]], PS[bks[1]], PSt[bks[1]]
                  Ec, Ect, Ed, Edt = PS[bks[2]], PSt[bks[2]], PS[bks[3]], PSt[bks[3]]
                  for k in range(8):
                      mm(Ea[:, 0:N], WS[wa][:, k, cs], hT[:, k, 0:N], k == 0, k == 7, [hTt, WSt[wa]], [Eat])
                  for k in range(8):
                      mm(Ec[:, 0:N], WS[wa][:, k, cs2], hT[:, k, 0:N], k == 0, k == 7, [hTt, WSt[wa]], [Ect])
                  for k in range(4):
                      mm(Eb[:, 0:N], WS[wb][:, k, cs], aT[:, k, 0:N], k == 0, k == 3, [aTt, WSt[wb]], [Ebt])
                  for k in range(4):
                      mm(Ed[:, 0:N], WS[wb][:, 4 + k, cs], hmT[:, k, 0:N], k == 0, k == 3, [hmTt, WSt[wb]], [Edt])
                  sa, sat = sc()
                  act(sa[:, 0:N], Ea[:, 0:N], AF.Sigmoid, [Eat, tconst], [sat], bias=CV[:, C_BFE + 4 + f:C_BFE + 5 + f])
                  t1, t1t = sc()
                  tt(t1[:, 0:N], sa[:, 0:N], Eb[:, 0:N], ALU.mult, [sat, Ebt], [t1t])
                  sb_, sbt = sc()
                  act(sb_[:, 0:N], Ec[:, 0:N], AF.Sigmoid, [Ect, tconst], [sbt], bias=CV[:, C_BFE + 12 + f:C_BFE + 13 + f])
                  t2, t2t = sc()
                  tt(t2[:, 0:N], sb_[:, 0:N], Ed[:, 0:N], ALU.mult, [sbt, Edt], [t2t])
                  tt(mgT[:, f, 0:N], t1[:, 0:N], t2[:, 0:N], ALU.add, [t1t, t2t], [mgTt])

          chk("E", gi)
          ftiles = [t for t in tiles if t.ffn]

          def proj_norm_res(pieces_for_c, lhs_of, nk, gvec, lhst, after_tile=None):
              gsl = []
              for c_ in range(2):
                  gsl.append(bpiece(gvec[c_ * 512:(c_ + 1) * 512]))
                  kc = 0
                  for (n_, src) in pieces_for_c(c_):
                      wi_ = wpiece([(0, n_, 0, 512, src)])
                      for k in range(n_):
                          for t in ftiles:
                              bk = 4 * c_ + t.slot
                              mm(PS[bk][0:t.T, :], lhs_of(kc, t), WS[wi_][:, k, :], kc == 0, kc == nk - 1,
                                 list(lhst) + [WSt[wi_]], [PSt[bk]])
                          kc += 1
                  for t in ftiles:
                      T = t.T
                      bk = 4 * c_ + t.slot
                      XN, XNt, SMt = XNs[t.slot % 2], XNts[t.slot % 2], SMts[t.slot]
                      act(XN[0:T, 0:512], PS[bk][0:T, :], AF.Square, [PSt[bk]], [XNt, SMt],
                          accum_out=SM[0:T, t.slot, 14 + c_:15 + c_])
              for t in ftiles:
                  T = t.T
                  SMt = SMts[t.slot]
                  rc = SM[0:T, t.slot, 16:17]
                  tt(rc, SM[0:T, t.slot, 14:15], SM[0:T, t.slot, 15:16], ALU.add, [SMt], [SMt])
                  rstd_from_ss(rc, rc, float(D), [SMt], [SMt])
                  for c_ in range(2):
                      bk = 4 * c_ + t.slot
                      tmp, tmpt = sc()
                      stt(tmp[0:T, :], PS[bk][0:T, :], rc, BS[gsl[c_]][0:T, :], ALU.mult, ALU.mult,
                          [PSt[bk], SMt, BSt[gsl[c_]]], [tmpt])
                      tt(XT[t.slot][0:T, c_ * 512:(c_ + 1) * 512], XT[t.slot][0:T, c_ * 512:(c_ + 1) * 512], tmp[0:T, :],
                         ALU.add, [tmpt, XTt[t.slot]], [XTt[t.slot]], eng="pool")
                  if after_tile is not None:
                      after_tile(t)

          if ftiles:
              chk("Epre", gi)
              warm_ln()
              proj_norm_res(lambda c_: [(8, ("w_out", 0, c_ * 512))],
                            lambda kc, t: mgT[:, kc, t.c0:t.c0 + t.T], 8, g_pm, [mgTt])
              for i0 in range(0, len(ftiles), 2):
                  rr([norm_to_T(XT[t.slot], XTt[t.slot], t, C_GPF, hT, hTt) for t in ftiles[i0:i0 + 2]])

              chk("F", gi)
              if gi + 1 < len(groups):
                  for t2 in groups[gi + 1]:
                      if t2.seq.name == "p" and t2.idx >= 1:
                          sl = groups[gi + 1].index(t2)
                          kb.dma("sp", STG[sl][0:t2.T, :], xp[(t2.idx - 1) * 128:t2.idx * 128, :], STGt[sl], True)
                          staged[(gi + 1, sl)] = True
              fa, fb = ftiles[0].c0, ftiles[-1].c0 + ftiles[-1].T
              Nf = fb - fa
              bankp = [(PA[0], PAt[0], PA[1], PAt[1]), (PA[2], PAt[2], PA[3], PAt[3]), (PM[0], PMt[0], PM[1], PMt[1])]
              for st in range(11):
                  wi = wpiece([(0, 8, 0, 256, ("w_gu", 0, st * 256)), (0, 8, 256, 256, ("w_gu", 0, DFF + st * 256))])
                  for jj in range(2):
                      f = st * 2 + jj
                      G_, Gt_, U_, Ut_ = bankp[f % 3]
                      for k in range(8):
                          mm(G_[:, 0:Nf], WS[wi][:, k, jj * 128:(jj + 1) * 128], hT[:, k, fa:fb], k == 0, k == 7,
                             [hTt, WSt[wi]], [Gt_])
                      for k in range(8):
                          mm(U_[:, 0:Nf], WS[wi][:, k, 256 + jj * 128:256 + (jj + 1) * 128], hT[:, k, fa:fb],
                             k == 0, k == 7, [hTt, WSt[wi]], [Ut_])
                      sg, sgt = sc()
                      act(sg[:, 0:Nf], G_[:, 0:Nf], AF.Silu, [Gt_], [sgt])
                      tt(hidT[:, f, fa:fb], sg[:, 0:Nf], U_[:, 0:Nf], ALU.mult, [sgt, Ut_], [U1t, QTt])
              warm_ln()

              def store_y(t):
                  if t.seq.name == "p":
                      dst = o_yp[(t.idx - 1) * 128:t.idx * 128, :]
                  else:
                      dst = o_ys
                  kb.dma("sp", dst, XT[t.slot][0:t.T, :], XTt[t.slot], False)

              proj_norm_res(lambda c_: [(8, ("w_dn", 0, c_ * 512)), (8, ("w_dn", 1024, c_ * 512)),
                                        (6, ("w_dn", 2048, c_ * 512))],
                            lambda kc, t: hidT[:, kc, t.c0:t.c0 + t.T], 22, g_pf, [U1t, QTt], after_tile=store_y)
          if gi + 1 < len(groups):
              memset(Vm[:, :, :, 128:129], 1.0, [U1t], eng="dve")

    except StopBuild:
        pass
    kb.finish()
    kb.sbuf_left = nc.sbuf_bytes_remaining
    es.close()
    return nc, kb


def host_inputs(inp, i, npt=32):
    f = lambda a: np.ascontiguousarray(a, dtype=np.float32)
    w_in = inp["w_in"][0]
    b_in = inp["b_in"][0]
    cvec = np.zeros((128, 68), np.float32)
    cvec[:, 0:8] = inp["g_pre_mix"][0].reshape(8, 128).T
    cvec[:, 8:16] = inp["g_pre_ffn"][0].reshape(8, 128).T
    cvec[:, 16:20] = inp["g_fq"][0].reshape(4, 128).T
    cvec[:, 20:24] = b_in[1544:2056].reshape(4, 128).T
    cvec[:, 24:32] = b_in[3088:4112].reshape(8, 128).T
    cvec[:, 32:40] = b_in[4112:5136].reshape(8, 128).T
    cvec[:, 40:44] = inp["g_mnorm"][0].reshape(4, 128).T
    cvec[:, 44:48] = inp["skip_m"][0].reshape(4, 128).T
    cvec[:, 48:52] = inp["b_conv"][0].reshape(4, 128).T
    cvec[:, 52:68] = inp["w_conv"][0].reshape(4, 4, 128).transpose(2, 1, 0).reshape(128, 16)
    gbias = np.zeros((8, 3), np.float32)
    gbias[:, 0] = b_in[1536:1544]
    gbias[0:4, 1] = b_in[3080:3084]
    gbias[0:4, 2] = b_in[3084:3088]
    return {
        "xp": f(inp["x_prompt"][i][:npt * 128]), "xs": f(inp["x_sample"][i]), "meta": f(inp["meta_tokens"]),
        "ck": f(inp["cache_fox_k"][0, i].reshape(PAST, 512)), "cv": f(inp["cache_fox_v"][0, i].reshape(PAST, 512)),
        "clf": f(inp["cache_fox_logf"][0, i]),
        "sC": f(inp["state_mlstm_C"][0, i]), "snT": f(inp["state_mlstm_n"][0, i].T),
        "sm": f(inp["state_mlstm_m"][0, i].reshape(4, 1)),
        "sconvT": f(inp["state_mlstm_conv"][0, i].reshape(3, 4, 128).transpose(2, 1, 0)),
        "w_in": f(w_in), "b_in": f(b_in),
        "w_g16": f(np.concatenate([w_in[:, 1536:1544], w_in[:, 3080:3088]], axis=1)),
        "gbias": gbias, "cvec": cvec,
        "g_fk": f(inp["g_fk"][0].reshape(512)), "g_pm": f(inp["g_post_mix"][0]), "g_pf": f(inp["g_post_ffn"][0]),
        "w_mq": f(inp["w_mq"][0]), "w_mk": f(inp["w_mk"][0]),
        "w_fp": f(inp["w_fox_proj"][0]), "w_mp": f(inp["w_ml_proj"][0]), "w_out": f(inp["w_out"][0]),
        "w_gu": f(inp["w_gate_up"][0]), "w_dn": f(inp["w_down"][0]),
    }


def assemble(results, npt=32):
    n = len(results)
    LP = NMETA + npt * 128
    st = lambda k: np.stack([np.asarray(r[k], dtype=np.float32) for r in results])
    conv = lambda k: st(k).transpose(0, 3, 2, 1).reshape(n, 3, 512)
    return (
        st("o_yp"), st("o_ys"),
        st("o_kp").reshape(1, n, LP, 8, 64), st("o_vp").reshape(1, n, LP, 8, 64), st("o_lfp").reshape(1, n, LP, 8),
        st("o_Cp").reshape(1, n, 4, 128, 128), st("o_np").reshape(n, 128, 4).transpose(0, 2, 1).reshape(1, n, 4, 128),
        st("o_mp").reshape(1, n, 4), conv("o_cvp").reshape(1, n, 3, 512),
        st("o_ks").reshape(1, n, DEC, 8, 64), st("o_vs").reshape(1, n, DEC, 8, 64), st("o_lfs").reshape(1, n, DEC, 8),
        st("o_Cs").reshape(1, n, 4, 128, 128), st("o_ns").reshape(n, 128, 4).transpose(0, 2, 1).reshape(1, n, 4, 128),
        st("o_ms").reshape(1, n, 4), conv("o_cvs").reshape(1, n, 3, 512),
    )


_NC_CACHE = {}


def kernel(**inputs):
    inp = {k: np.asarray(v) for k, v in inputs.items()}
    if 32 not in _NC_CACHE:
        _NC_CACHE[32] = build(32)[0]
    nc = _NC_CACHE[32]
    in_maps = [host_inputs(inp, i, 32) for i in range(8)]
    res = run_bass_kernel_spmd(nc, in_maps, core_ids=list(range(8)))
    return assemble(res.results, 32)
```

```python
import numpy as np
from contextlib import ExitStack
import concourse.bass as bass
import concourse.mybir as mybir
from concourse.bass_utils import run_bass_kernel_spmd

F32, BF16 = mybir.dt.float32, mybir.dt.bfloat16
AF = mybir.ActivationFunctionType
ALU = mybir.AluOpType
AX = mybir.AxisListType

D = 1024
SEQ = 4096
NMETA = 16
DEC = 64
PAST = 1024
DFF = 2816
EPS = 1e-6
NEG = -30000.0
SEM_LIMIT = 30000
SEQ_CD = False


class StopBuild(Exception):
    pass


class Tok:
    __slots__ = ("w", "r", "d", "name", "excl")

    def __init__(self, name="", excl=False):
        self.excl = excl
        self.w = {}
        self.r = {}
        self.d = {}
        self.name = name


class Eng:
    def __init__(self, kb, name, handle):
        self.kb, self.name, self.h = kb, name, handle
        self.sem = kb.sem(name + "_s0")
        self.cnt = 0
        self.epoch = 0
        self.seen = {}

    def wait(self, sem, val):
        if self.seen.get(sem, 0) >= val:
            return
        self.h.wait_ge(sem, val)
        self.seen[sem] = val

    def bump(self):
        if self.cnt >= SEM_LIMIT:
            self.epoch += 1
            self.sem = self.kb.sem("%s_s%d" % (self.name, self.epoch))
            self.cnt = 0
        self.cnt += 1
        return (self.sem, self.cnt)


class KB:
    def __init__(self, nc, es):
        self.nc, self.es = nc, es
        self.nsem = 0
        self.E = {}
        for name, h in (("pe", nc.tensor), ("act", nc.scalar), ("dve", nc.vector),
                        ("pool", nc.gpsimd), ("sp", nc.sync)):
            self.E[name] = Eng(self, name, h)
        self.dma_toks = []
        self.ninst = 0
        self.stop_at = 0

    def sem(self, name):
        self.nsem += 1
        return self.es.enter_context(self.nc.semaphore(name))

    def sb(self, name, shape, dt):
        return self.es.enter_context(self.nc.sbuf_tensor(name, list(shape), dt))

    def ps(self, name, shape, dt):
        return self.es.enter_context(self.nc.psum_tensor(name, list(shape), dt))

    def _deps(self, e, rd, wr):
        own = e.name == "pe"
        for b in rd:
            for s_, v in b.w.items():
                if not (own and s_ is e.sem):
                    e.wait(s_, v)
            if b.excl:
                for s_, v in b.r.items():
                    if s_ is not e.sem:
                        e.wait(s_, v)
        for b in wr:
            for s_, v in b.w.items():
                if not (own and s_ is e.sem):
                    e.wait(s_, v)
            for s_, v in b.r.items():
                if own and s_ is e.sem:
                    continue
                e.wait(s_, v)

    def prewait(self, eng, rd=(), wr=()):
        self._deps(self.E[eng], rd, wr)

    def op(self, eng, fn, rd=(), wr=()):
        if self.stop_at and self.ninst >= self.stop_at:
            raise StopBuild()
        e = self.E[eng]
        self._deps(e, rd, wr)
        ins = fn(e.h)
        tag = e.bump()
        ins.then_inc(tag[0], 1)
        self.ninst += 1
        for b in rd:
            if b.r.get(tag[0], 0) < tag[1]:
                b.r[tag[0]] = tag[1]
        for b in wr:
            b.w = {tag[0]: tag[1]}
            b.r = {}

    def dma(self, q, out, in_, tok, load, rd=(), wr=()):
        e = self.E[q]
        if load:
            self._deps(e, rd, (tok,) + tuple(wr))
        else:
            self._deps(e, (tok,) + tuple(rd), wr)
        if q not in tok.d:
            tok.d[q] = [self.sem("d%s_%s" % (q, tok.name)), 0]
            if tok not in self.dma_toks:
                self.dma_toks.append(tok)
        ds = tok.d[q]
        ds[1] += 16
        e.h.dma_start(out=out, in_=in_).then_inc(ds[0], 16)
        self.ninst += 1
        if load:
            tok.w[ds[0]] = ds[1]
            tok.r = {}
            for b in wr:
                b.w[ds[0]] = ds[1]
                b.r = {}
        else:
            tok.r[ds[0]] = ds[1]
        for b in rd:
            b.r[ds[0]] = ds[1]

    def finish(self):
        e = self.E["sp"]
        for t in self.dma_toks:
            for ds in t.d.values():
                e.wait(ds[0], ds[1])
        for name, o in self.E.items():
            if name != "sp" and o.cnt > 0:
                e.wait(o.sem, o.cnt)


class Tile:
    pass


def build(npt=32, stop=None):
    assert npt % 4 == 0
    LP = NMETA + npt * 128
    NVT = max(npt + 1, 10)
    nc = bass.Bass("TRN2", target_bir_lowering=False)
    es = ExitStack()
    kb = KB(nc, es)

    def din(name, shape):
        return nc.dram_tensor(name, list(shape), F32, kind="ExternalInput").ap()

    def dout(name, shape):
        return nc.dram_tensor(name, list(shape), F32, kind="ExternalOutput").ap()

    xp = din("xp", [npt * 128, D]); xs = din("xs", [DEC, D]); meta = din("meta", [NMETA, D])
    ck = din("ck", [PAST, 512]); cv = din("cv", [PAST, 512]); clf = din("clf", [PAST, 8])
    sC = din("sC", [4, 128, 128]); snT = din("snT", [128, 4]); sm = din("sm", [4, 1])
    sconvT = din("sconvT", [128, 4, 3])
    w_in = din("w_in", [D, 5136]); b_in = din("b_in", [5136])
    w_g16 = din("w_g16", [D, 16]); gbias = din("gbias", [8, 3]); cvec = din("cvec", [128, 68])
    g_fk = din("g_fk", [512]); g_pm = din("g_pm", [D]); g_pf = din("g_pf", [D])
    w_mq = din("w_mq", [4, 128, 128]); w_mk = din("w_mk", [4, 128, 128])
    w_fp = din("w_fp", [512, D]); w_mp = din("w_mp", [512, D]); w_out = din("w_out", [D, D])
    w_gu = din("w_gu", [D, 2 * DFF]); w_dn = din("w_dn", [DFF, D])

    o_yp = dout("o_yp", [npt * 128, D]); o_ys = dout("o_ys", [DEC, D])
    o_kp = dout("o_kp", [LP, 512]); o_vp = dout("o_vp", [LP, 512]); o_lfp = dout("o_lfp", [LP, 8])
    o_Cp = dout("o_Cp", [4, 128, 128]); o_np = dout("o_np", [128, 4, 1]); o_mp = dout("o_mp", [4, 1])
    o_cvp = dout("o_cvp", [128, 4, 3])
    o_ks = dout("o_ks", [DEC, 512]); o_vs = dout("o_vs", [DEC, 512]); o_lfs = dout("o_lfs", [DEC, 8])
    o_Cs = dout("o_Cs", [4, 128, 128]); o_ns = dout("o_ns", [128, 4, 1]); o_ms = dout("o_ms", [4, 1])
    o_cvs = dout("o_cvs", [128, 4, 3])

    sb, ps = kb.sb, kb.ps
    NK = NMETA + (NVT - 1) * 128
    KT = sb("KT", [128, 4, NK], BF16)
    VA = sb("VA", [128, NVT + 1, 8, 65], BF16)
    VAf = VA[:, :, :, :].rearrange("p t h c -> p (t h c)")
    FN = sb("FN", [128, NVT, 8], F32)
    KVt = [Tok("kv%d" % i) for i in range(NVT)]

    identb = sb("identb", [128, 128], BF16); identf = sb("identf", [128, 128], F32)
    UT = sb("UT", [128, 128], F32); MASKN = sb("MASKN", [128, 128], BF16)
    ONESF = sb("ONESF", [128, 512], F32)
    sel8 = sb("sel8", [128, 8, 128], BF16); sel4 = sb("sel4", [4, 4, 128], F32)
    CV = sb("CVEC", [128, 68], F32); GBI = sb("GBI", [8, 3], F32); NGB = sb("NGB", [8, 3], F32)
    GFQ = sb("GFQ", [128, 4], F32)
    WQ = sb("WQ", [128, 4, 128], BF16); WK = sb("WK", [128, 4, 128], BF16)
    WG = sb("WG", [128, 8, 16], BF16)
    tconst = Tok("const")
    C_GPM, C_GPF, C_GFQ, C_BFE, C_GMN, C_SKP, C_BCV, C_WCV = 0, 8, 16, 20, 40, 44, 48, 52

    NWS = 3
    WS = [sb("WS%d" % i, [128, 8, 512], BF16) for i in range(NWS)]
    WSt = [Tok("ws%d" % i) for i in range(NWS)]
    BS = [sb("BS%d" % i, [128, 512], F32) for i in range(2)]
    BSt = [Tok("bs%d" % i) for i in range(2)]
    XT = [sb("XT%d" % i, [128, D], F32) for i in range(4)]
    XTt = [Tok("xt%d" % i) for i in range(4)]
    hT = sb("hT", [128, 8, 512], BF16); hTt = Tok("hT")
    U1 = sb("U1", [128, 11264], BF16); U1t = Tok("U1")
    QTt = Tok("QTq")
    hidT = U1[:, 0:11264].rearrange("p (f n) -> p f n", f=22)
    QT = U1[:, 0:2048].rearrange("p (j n) -> p j n", j=4)
    FTq = U1[0:8, 2048:2560]
    FTq2 = U1[64:72, 2048:2560]
    FTqP = U1[0:64, 2048:2560]
    FTq2P = U1[64:128, 2048:2560]
    MXW = 518
    mxT = U1[:, 2560:2560 + 2 * 4 * MXW].bitcast(F32).rearrange("p (j n) -> p j n", j=4)
    o = 2560 + 2 * 4 * MXW
    Vm = U1[:, o:o + 4 * 4 * 129].rearrange("p (t h e) -> p t h e", t=4, h=4)
    o += 4 * 4 * 129
    OG = U1[:, o:o + 4 * 512].rearrange("p (t c) -> p t c", t=4)
    o += 4 * 512
    assert o <= 11264
    xcT = sb("xcT", [128, 4, 512], BF16); xcTt = Tok("xcT")
    aT = sb("aT", [128, 4, 512], BF16); aTt = Tok("aT")
    hmT = sb("hmT", [128, 4, 512], BF16); hmTt = Tok("hmT")
    mgT = sb("mgT", [128, 8, 512], BF16); mgTt = Tok("mgT")
    STG = [mgT[:, 0:4, :].rearrange("p k n -> p (k n)").bitcast(F32),
           mgT[:, 4:8, :].rearrange("p k n -> p (k n)").bitcast(F32),
           aT[:, :, :].rearrange("p k n -> p (k n)").bitcast(F32),
           hmT[:, :, :].rearrange("p k n -> p (k n)").bitcast(F32)]
    STGt = [mgTt, mgTt, aTt, hmTt]
    staged = {}
    NSC = 4
    SC = [sb("SC%d" % i, [128, 512], F32) for i in range(NSC)]
    SCt = [Tok("sc%d" % i) for i in range(NSC)]
    sc_i = [0]

    def sc():
        i = sc_i[0] % NSC
        sc_i[0] += 1
        return SC[i], SCt[i]

    XNs = [sb("XN%d" % i, [128, D], BF16) for i in range(2)]; XNts = [Tok("XN%d" % i) for i in range(2)]
    PTB = [sb("PTB%d" % i, [128, 512], BF16) for i in range(4)]
    PTBt = [Tok("ptb%d" % i) for i in range(4)]
    GR = [sb("GR%d" % i, [128, 512], F32) for i in range(4)]
    GRt = Tok("GR")
    GC = sb("GC", [128, 4, 16], F32); GCt = Tok("GC")
    DEC_ = sb("DECr", [4, 8], F32); DECB = sb("DECB", [128, 4, 4], F32); DECt = Tok("DEC")
    SM = sb("SM", [128, 4, 32], F32); SMts = [Tok("SM%d" % i) for i in range(4)]
    QTs = sb("QTs", [128, 128], BF16); KTs = sb("KTs", [128, 128], BF16); KKs = sb("KKs", [128, 128], BF16)
    WTs = sb("WTs", [128, 128], F32); WMs = WTs; ATs = sb("ATs", [128, 128], BF16)
    VTs = sb("VTs", [128, 132], BF16); INs = sb("INs", [128, 132], F32); NUMs = sb("NUMs", [128, 132], F32)
    HH = sb("HH", [128, 512], F32)
    mlt = {n: Tok(n) for n in ("QTs", "KTs", "KKs", "WTs", "WMs", "ATs", "VTs", "INs", "NUMs", "HH")}

    PS = [ps("PS%d" % i, [128, 512], F32) for i in range(8)]
    PSt = [Tok("ps%d" % i, excl=True) for i in range(8)]
    PA, PAt = PS[0:4], PSt[0:4]
    PM, PMt = PS[4:6], PSt[4:6]
    PSB = [PS[i][:, :].bitcast(BF16).rearrange("p (k t) -> p k t", k=8) for i in range(8)]
    TRP = [PSB[6], PSB[7]]
    TRPt = PSt[6:8]
    trp_i = [0]

    class SeqS:
        pass
    seqs = {}
    for nm in ("p", "s"):
        s = SeqS()
        s.name = nm
        s.C = sb("C_" + nm, [128, 4, 129], F32); s.Cb = sb("Cb_" + nm, [128, 4, 129], BF16)
        s.Ct = Tok("C" + nm); s.Cbt = Tok("Cb" + nm)
        s.car = sb("car_" + nm, [128, 8], F32)
        s.cart = Tok("car" + nm)
        s.HALO = sb("HALO_" + nm, [128, 4, 3], F32); s.halot = Tok("halo" + nm)
        seqs[nm] = s

    op = kb.op

    def act(out, in_, func, rd, wr, **kw):
        op("act", lambda e: e.activation(out=out, in_=in_, func=func, **kw), rd, wr)

    def tt(out, in0, in1, o_, rd, wr, eng="dve"):
        op(eng, lambda e: e.tensor_tensor(out=out, in0=in0, in1=in1, op=o_), rd, wr)

    def ts(out, in0, s1, s2, o0, o1, rd, wr, eng="dve"):
        if s2 is None:
            op(eng, lambda e: e.tensor_scalar(out=out, in0=in0, scalar1=s1, scalar2=None, op0=o0), rd, wr)
        else:
            op(eng, lambda e: e.tensor_scalar(out=out, in0=in0, scalar1=s1, scalar2=s2, op0=o0, op1=o1), rd, wr)

    def stt(out, in0, scalar, in1, o0, o1, rd, wr, eng="dve"):
        op(eng, lambda e: e.scalar_tensor_tensor(out=out, in0=in0, scalar=scalar, in1=in1, op0=o0, op1=o1), rd, wr)

    def cp(out, in_, rd, wr, eng="dve"):
        op(eng, lambda e: e.tensor_copy(out=out, in_=in_), rd, wr)

    def mm(out, lhsT, rhs, start, stop, rd, wr, skip=False):
        if skip:
            op("pe", lambda e: e.matmul(out, lhsT, rhs, start=start, stop=stop, skip_group_check=True), rd, wr)
        else:
            op("pe", lambda e: e.matmul(out, lhsT, rhs, start=start, stop=stop), rd, wr)

    def tr(out, in_, ident, rd, wr):
        op("pe", lambda e: e.transpose(out=out, in_=in_, identity=ident), rd, wr)

    def memset(ap, val, wr, eng="pool"):
        op(eng, lambda e: e.memset(ap, val), (), wr)

    def asel(out, pattern, cmp_, fill, base, cm, wr):
        op("pool", lambda e: e.affine_select(out=out, in_=out, pattern=pattern, compare_op=cmp_,
                                             fill=fill, base=base, channel_multiplier=cm), wr, wr)

    def rstd_from_ss(dst, ss, n, rd, wr):
        ts(dst, ss, 1.0 / n, EPS, ALU.mult, ALU.add, rd, wr)
        act(dst, dst, AF.Ln, wr, wr)
        act(dst, dst, AF.Exp, wr, wr, scale=-0.5)

    warmt = Tok("warm")

    def warm_ln():
        act(SM[0:1, 3, 31:32], ONESF[0:1, 0:1], AF.Ln, [tconst], [warmt])

    ws_i = [0]

    WSRC = {"w_in": w_in, "w_fp": w_fp, "w_mp": w_mp, "w_out": w_out, "w_gu": w_gu, "w_dn": w_dn}
    conv = {}

    def conv_get(name, r0, nrows, c0, ncol):
        key = (name, r0, nrows, c0, ncol)
        if key not in conv:
            scr = nc.dram_tensor("scr_%s_%d_%d_%d" % (name, r0, c0, ncol), [nrows, ncol], BF16).ap()
            tk = Tok("cv%d" % len(conv))
            kb.dma("pool", scr, WSRC[name][r0:r0 + nrows, c0:c0 + ncol], tk, True)
            conv[key] = (scr, tk)
        return conv[key]

    cur_g = [0]

    def wpiece(parts, direct=False):
        i = ws_i[0] % NWS
        ws_i[0] += 1
        for (k0, nk, c0, ncol, (name, r0, sc0)) in parts:
            scr, tk = conv_get(name, r0, nk * 128, sc0, ncol)
            if direct and cur_g[0] == 0:
                kb.dma("pool", WS[i][:, k0:k0 + nk, c0:c0 + ncol],
                       WSRC[name][r0:r0 + nk * 128, sc0:sc0 + ncol].rearrange("(k p) c -> p k c", p=128), WSt[i], True)
                continue
            kb.dma("pool", WS[i][:, k0:k0 + nk, c0:c0 + ncol],
                   scr.rearrange("(k p) c -> p k c", p=128), WSt[i], True, rd=[tk])
        return i

    def pc_B():
        for col0 in (0, 512, 1024, 2056, 2568, 1544):
            conv_get("w_in", 0, 1024, col0, 512)

    def pc_E():
        for cb in range(4):
            conv_get("w_in", 0, 1024, 3088 + cb * 256, 256)
            conv_get("w_in", 0, 1024, 4112 + cb * 256, 256)
            conv_get("w_fp", 0, 512, cb * 256, 256)
            conv_get("w_mp", 0, 512, cb * 256, 256)

    def pc_F():
        for c_ in range(2):
            conv_get("w_out", 0, 1024, c_ * 512, 512)

    def pc_Gu():
        for st in range(11):
            conv_get("w_gu", 0, 1024, st * 256, 256)
            conv_get("w_gu", 0, 1024, DFF + st * 256, 256)

    def pc_Gd():
        for c_ in range(2):
            conv_get("w_dn", 0, 1024, c_ * 512, 512)
            conv_get("w_dn", 1024, 1024, c_ * 512, 512)
            conv_get("w_dn", 2048, 768, c_ * 512, 512)

    bs_i = [0]

    def bpiece(src_vec):
        i = bs_i[0] % 2
        bs_i[0] += 1
        n = src_vec.shape[0]
        kb.dma("sp", BS[i][:, 0:n], src_vec.partition_broadcast(128), BSt[i], True)
        return i

    memset(identf[:], 1.0, [tconst]); asel(identf[:], [[-1, 128]], ALU.is_equal, 0.0, 0, 1, [tconst])
    cp(identb[:], identf[:], [tconst], [tconst], eng="pool")
    memset(UT[:], 1.0, [tconst]); asel(UT[:], [[1, 128]], ALU.is_ge, 0.0, 0, -1, [tconst])
    memset(MASKN[:], NEG, [tconst]); asel(MASKN[:], [[-1, 128]], ALU.is_gt, 0.0, 0, 1, [tconst])
    memset(ONESF[:], 1.0, [tconst])
    memset(sel8[:], 0.0, [tconst])
    memset(sel8[0:8], 1.0, [tconst]); asel(sel8[0:8], [[-1, 8], [0, 128]], ALU.is_equal, 0.0, 0, 1, [tconst])
    cp(sel8[64:72], sel8[0:8], [tconst], [tconst])
    memset(sel4[:], 1.0, [tconst]); asel(sel4[:], [[-1, 4], [0, 128]], ALU.is_equal, 0.0, 0, 1, [tconst])
    memset(VA[:, :, :, :], 0.0, KVt)
    memset(VA[:, :, :, 64:65], 1.0, KVt)
    memset(Vm[:, :, :, 128:129], 1.0, [U1t])
    kb.dma("sp", CV[:], cvec, tconst, True)
    kb.dma("sp", GBI[:], gbias, tconst, True)
    kb.dma("pool", WQ[:], w_mq.rearrange("h d e -> d h e"), tconst, True)
    kb.dma("pool", WK[:], w_mk.rearrange("h d e -> d h e"), tconst, True)
    kb.dma("pool", WG[:], w_g16.rearrange("(k p) c -> p k c", p=128), tconst, True)
    ts(NGB[:], GBI[:], -1.0, None, ALU.mult, None, [tconst], [tconst])
    ts(GFQ[:], CV[:, C_GFQ:C_GFQ + 4], 0.125, None, ALU.mult, None, [tconst], [tconst])
    P_, S_ = seqs["p"], seqs["s"]
    memset(P_.C[:], 0.0, [P_.Ct]); memset(P_.Cb[:], 0.0, [P_.Cbt]); memset(P_.car[:], 0.0, [P_.cart])
    memset(P_.HALO[:], 0.0, [P_.halot])
    memset(S_.car[:], 0.0, [S_.cart])

    SK0 = NK - (PAST + DEC)
    SV0 = NVT - 9
    assert SK0 >= NMETA

    def kt_toks(lo, hi):
        res = []
        for i in range(NVT):
            a = 0 if i == 0 else NMETA + (i - 1) * 128
            b = NMETA if i == 0 else a + 128
            if a < hi and lo < b:
                res.append(KVt[i])
        return res

    for j in range(8):
        kb.dma("pool", VA[:, SV0 + j, :, 0:64], cv[j * 128:(j + 1) * 128, :].rearrange("p (h d) -> p h d", h=8),
               tconst, True, wr=[KVt[SV0 + j]])
    wsl = ws_i[0] % NWS
    ws_i[0] += 1
    kb.dma("pool", WS[wsl][:], ck.rearrange("(j p) c -> p j c", p=128), WSt[wsl], True)
    for j in range(8):
        b = trp_i[0] % 2
        trp_i[0] += 1
        for q in range(4):
            tr(TRP[b][:, q, :], WS[wsl][:, j, q * 128:(q + 1) * 128], identb[:], [WSt[wsl], tconst], [TRPt[b]])
        lo = SK0 + j * 128
        cp(KT[:, :, lo:lo + 128], TRP[b][:, 0:4, :], [TRPt[b]], kt_toks(lo, lo + 128))
    CL = SC[0]; CLt = SCt[0]
    kb.dma("sp", CL[:, 0:64].rearrange("p (j h) -> p j h", j=8), clf.rearrange("(j p) h -> p j h", p=128), CLt, True)
    ts(CL[:, 0:64], CL[:, 0:64], -1.0, None, ALU.mult, None, [CLt], [CLt])
    for j in range(8):
        for jj in range(j + 1):
            lhs = UT[:] if jj == j else ONESF[:, 0:128]
            mm(PM[0][:, j * 8:(j + 1) * 8], lhs, CL[:, jj * 8:(jj + 1) * 8], jj == 0, jj == j,
               [CLt, tconst], [PMt[0]], skip=True)
    cp(FN[:, SV0:SV0 + 8, :], PM[0][:, 0:64].rearrange("p (j h) -> p j h", j=8), [PMt[0]], KVt[SV0:SV0 + 8])
    for j in range(8):
        mm(PM[1][0:8, 0:1], CL[:, j * 8:(j + 1) * 8], ONESF[:, 0:1], j == 0, j == 7, [CLt, tconst], [PMt[1]])
    cp(S_.car[32:40, 0:1], PM[1][0:8, 0:1], [PMt[1]], [S_.cart])
    kb.dma("sp", S_.C[:, :, 0:128], sC.rearrange("h k v -> k h v"), S_.Ct, True)
    NTs = sb("NTs", [128, 8], F32); NTst = Tok("NTs")
    kb.dma("sp", NTs[:, 0:4], snT, NTst, True)
    cp(S_.C[:, :, 128:129], NTs[:, 0:4].unsqueeze(2), [NTst], [S_.Ct])
    cp(S_.Cb[:], S_.C[:], [S_.Ct], [S_.Cbt])
    kb.dma("sp", S_.car[0:4, 2:3], sm, S_.cart, True)
    cp(S_.car[0:4, 3:4], S_.car[0:4, 2:3], [S_.cart], [S_.cart])
    kb.dma("sp", S_.HALO[:], sconvT, S_.halot, True)

    pc_B()

    def mk_tile(seq, T, idx):
        t = Tile()
        t.seq, t.T, t.idx = seqs[seq], T, idx
        if seq == "p":
            t.vi = idx
            t.k0 = 0 if idx == 0 else NMETA + (idx - 1) * 128
            t.first, t.last = idx == 0, idx == npt
        else:
            t.vi = SV0 + 8
            t.k0 = SK0 + PAST
            t.first, t.last = True, True
        t.ffn = not (seq == "p" and idx == 0)
        return t

    groups = [[mk_tile("p", NMETA, 0), mk_tile("p", 128, 1), mk_tile("p", 128, 2), mk_tile("s", DEC, 0)]]
    nxt = 3
    while nxt <= npt:
        n_ = min(4, npt - nxt + 1)
        groups.append([mk_tile("p", 128, nxt + i) for i in range(n_)])
        nxt += n_

    if isinstance(stop, int):
        kb.stop_at = stop

    kb.marks = []

    def chk(name, gi):
        kb.marks.append((gi, name, kb.E["pe"].cnt + kb.E["pe"].epoch * SEM_LIMIT))
        if stop is not None and stop == "%s%d" % (name, gi):
            raise StopBuild()

    try:
      for gi, tiles in enumerate(groups):
          chk("start", gi)
          cur_g[0] = gi
          c = 0
          mc = 0
          prev_seq = None
          for i, t in enumerate(tiles):
              t.slot = i
              t.c0 = c
              if t.seq is not prev_seq:
                  mc += 3
                  prev_seq = t.seq
              t.m0 = mc
              c += t.T
              mc += t.T
          N = c
          runs = []
          for t in tiles:
              if runs and runs[-1][0] is t.seq:
                  runs[-1][2] = t.c0 + t.T
                  runs[-1][4].append(t)
              else:
                  runs.append([t.seq, t.c0, t.c0 + t.T, t.m0, [t]])

          def rr(gens):
              gens = list(gens)
              while gens:
                  for g_ in list(gens):
                      try:
                          next(g_)
                      except StopIteration:
                          gens.remove(g_)

          def norm_to_T(src, srct, t, gcol, dstT, dstt):
              T = t.T
              XN, XNt, SMt = XNs[t.slot % 2], XNts[t.slot % 2], SMts[t.slot]
              ssc = SM[0:T, t.slot, 0:1]
              rsc = SM[0:T, t.slot, 1:2]
              act(XN[0:T, :], src[0:T, :], AF.Square, [srct], [XNt, SMt], accum_out=ssc)
              yield
              ts(rsc, ssc, 1.0 / D, EPS, ALU.mult, ALU.add, [SMt], [SMt])
              yield
              act(rsc, rsc, AF.Ln, [SMt], [SMt])
              yield
              act(rsc, rsc, AF.Exp, [SMt], [SMt], scale=-0.5)
              yield
              ts(XN[0:T, :], src[0:T, :], rsc, None, ALU.mult, None, [srct, SMt], [XNt])
              yield
              b = t.slot % 2
              for k in range(8):
                  tr(TRP[b][:, k, 0:T], XN[0:T, k * 128:(k + 1) * 128], identb[0:T, 0:T], [XNt, tconst], [TRPt[b]])
              yield
              tt(dstT[:, :, t.c0:t.c0 + T], TRP[b][:, :, 0:T],
                 CV[:, gcol:gcol + 8].unsqueeze(2).broadcast_to([128, 8, T]), ALU.mult,
                 [TRPt[b], tconst], [dstt])
              yield

          for t in tiles:
              T = t.T
              if t.seq.name == "p":
                  src = meta if t.idx == 0 else xp[(t.idx - 1) * 128:t.idx * 128, :]
              else:
                  src = xs
              if (gi, t.slot) in staged:
                  kb.dma("sp", XT[t.slot][0:T, :], STG[t.slot][0:T, :], XTt[t.slot], True, rd=[STGt[t.slot]])
              else:
                  kb.dma("sp", XT[t.slot][0:T, :], src, XTt[t.slot], True)
          for i0 in range(0, len(tiles), 2):
              rr([norm_to_T(STG[t.slot] if (gi, t.slot) in staged else XT[t.slot],
                            STGt[t.slot] if (gi, t.slot) in staged else XTt[t.slot], t, C_GPM, hT, hTt)
                  for t in tiles[i0:i0 + 2]])

          chk("A", gi)
          if gi == 0:
              pc_E()
          def grow(a, b, n):
              return GR[a][b:b + n, :]
          IG, LF2, BB, MM = grow(0, 0, 4), grow(1, 0, 4), grow(2, 0, 4), grow(3, 0, 4)
          Lf, CSf = grow(0, 32, 8), grow(1, 32, 8)
          EM, WI, WKr = grow(0, 64, 4), grow(1, 64, 4), grow(2, 64, 4)
          UU, NM = IG, MM
          for k in range(8):
              mm(PM[0][0:8, 0:N], WG[:, k, 0:8], hT[:, k, 0:N], k == 0, k == 7, [hTt, tconst], [PMt[0]])
          for k in range(8):
              mm(PM[1][0:4, 0:N], WG[:, k, 8:12], hT[:, k, 0:N], k == 0, k == 7, [hTt, tconst], [PMt[1]])
          for k in range(8):
              mm(PS[6][0:4, 0:N], WG[:, k, 12:16], hT[:, k, 0:N], k == 0, k == 7, [hTt, tconst], [PSt[6]])
          act(Lf[:, 0:N], PM[0][0:8, 0:N], AF.Exp, [PMt[0], tconst], [GRt], scale=-1.0, bias=NGB[0:8, 0:1])
          act(Lf[:, 0:N], Lf[:, 0:N], AF.Ln, [GRt], [GRt], bias=1.0)
          act(IG[:, 0:N], PM[1][0:4, 0:N], AF.Identity, [PMt[1], tconst], [GRt], bias=GBI[0:4, 1:2])
          act(LF2[:, 0:N], PS[6][0:4, 0:N], AF.Exp, [PSt[6], tconst], [GRt], scale=-1.0, bias=NGB[0:4, 2:3])
          act(LF2[:, 0:N], LF2[:, 0:N], AF.Ln, [GRt], [GRt], bias=1.0)
          ts(LF2[:, 0:N], LF2[:, 0:N], -1.0, None, ALU.mult, None, [GRt], [GRt])
          for r in runs:
              s, a, b = r[0], r[1], r[2]
              op("dve", lambda e: e.tensor_tensor_scan(out=CSf[:, a:b], data0=ONESF[32:40, 0:b - a],
                                                       data1=Lf[:, a:b], initial=s.car[32:40, 0:1],
                                                       op0=ALU.mult, op1=ALU.add), [GRt, s.cart, tconst], [GRt])
              op("dve", lambda e: e.tensor_tensor_scan(out=BB[:, a:b], data0=ONESF[0:4, 0:b - a], data1=LF2[:, a:b],
                                                       initial=s.car[0:4, 1:2], op0=ALU.mult, op1=ALU.add),
                 [GRt, s.cart, tconst], [GRt])
              op("dve", lambda e: e.tensor_tensor_scan(out=MM[:, a:b], data0=LF2[:, a:b], data1=IG[:, a:b],
                                                       initial=s.car[0:4, 2:3], op0=ALU.add, op1=ALU.max),
                 [GRt, s.cart], [GRt])
          for r in runs:
              s, b = r[0], r[2]
              cp(s.car[32:40, 0:1], CSf[:, b - 1:b], [GRt], [s.cart])
              cp(s.car[0:4, 1:2], BB[:, b - 1:b], [GRt], [s.cart])
              cp(s.car[0:4, 2:3], MM[:, b - 1:b], [GRt], [s.cart])
          tt(UU[:, 0:N], IG[:, 0:N], BB[:, 0:N], ALU.subtract, [GRt], [GRt])
          act(EM[:, 0:N], MM[:, 0:N], AF.Exp, [GRt], [GRt], scale=-1.0)
          tt(NM[:, 0:N], BB[:, 0:N], MM[:, 0:N], ALU.subtract, [GRt], [GRt])
          memset(U1[:, 2048:2560], 0.0, [QTt], eng="dve")
          ts(FTq[:, 0:N], CSf[:, 0:N], -1.0, None, ALU.mult, None, [GRt], [QTt])
          ts(FTq2[:, 0:N], CSf[:, 0:N], -1.0, None, ALU.mult, None, [GRt], [QTt])
          def rr_until(gens):
              live = list(gens)
              while live:
                  for g_ in list(live):
                      try:
                          r_ = next(g_)
                      except StopIteration:
                          live.remove(g_)
                          continue
                      if r_ == "XN":
                          live.remove(g_)

          def tokmajor(col0, evac, bb, pend=None, defer=False):
              wi = wpiece([(0, 8, 0, 512, ("w_in", 0, col0))], direct=True)
              bi = bpiece(b_in[col0:col0 + 512])
              for k in range(8):
                  for t in tiles:
                      mm(PS[bb + t.slot][0:t.T, :], hT[:, k, t.c0:t.c0 + t.T], WS[wi][:, k, :], k == 0, k == 7,
                         [hTt, WSt[wi]], [PSt[bb + t.slot]])
              if pend:
                  for i0 in range(0, len(pend), 2):
                      rr(pend[i0:i0 + 2])
              gens = [evac(t, bi, bb + t.slot) for t in tiles]
              for i0 in range(0, len(gens), 2):
                  if defer:
                      rr_until(gens[i0:i0 + 2])
                  else:
                      rr(gens[i0:i0 + 2])
              return gens if defer else None

          def headnorm_g(Z, Zt, t, nh, hd, rcol):
              T = t.T
              SMt = SMts[t.slot]
              sq, sqt = sc()
              act(sq[0:T, :], Z[0:T, :], AF.Square, [Zt], [sqt])
              yield
              op("dve", lambda e: e.tensor_reduce(out=SM[0:T, t.slot, rcol:rcol + nh],
                                                  in_=sq[0:T, :].rearrange("p (h d) -> p h d", h=nh),
                                                  axis=AX.X, op=ALU.add), [sqt], [SMt])
              dst = SM[0:T, t.slot, rcol:rcol + nh]
              ts(dst, dst, 1.0 / hd, EPS, ALU.mult, ALU.add, [SMt], [SMt])
              yield
              act(dst, dst, AF.Ln, [SMt], [SMt])
              yield
              act(dst, dst, AF.Exp, [SMt], [SMt], scale=-0.5)
              yield

          def headnorm(Z, Zt, t, nh, hd, rcol):
              for _ in headnorm_g(Z, Zt, t, nh, hd, rcol):
                  pass

          def ev_fq(t, bi, bk):
              T = t.T
              XN, XNt, SMt = XNs[t.slot % 2], XNts[t.slot % 2], SMts[t.slot]
              z, zt = sc()
              tt(z[0:T, :], PS[bk][0:T, :], BS[bi][0:T, :], ALU.add, [PSt[bk], BSt[bi]], [zt])
              yield
              yield from headnorm_g(z, zt, t, 8, 64, 2)
              xo = (t.slot // 2) * 512
              tt(XN[0:T, xo:xo + 512].rearrange("p (h d) -> p h d", h=8), z[0:T, :].rearrange("p (h d) -> p h d", h=8),
                 SM[0:T, t.slot, 2:10].unsqueeze(2).broadcast_to([T, 8, 64]), ALU.mult, [zt, SMt], [XNt])
              yield "XN"
              for q in range(4):
                  tr(PSB[bk][:, q, 0:T], XN[0:T, xo + q * 128:xo + (q + 1) * 128], identb[0:T, 0:T], [XNt, tconst], [PSt[bk]])
              yield
              tt(QT[:, :, t.c0:t.c0 + T], PSB[bk][:, 0:4, 0:T], GFQ[:].unsqueeze(2).broadcast_to([128, 4, T]),
                 ALU.mult, [PSt[bk], tconst], [QTt])
              yield

          gfk_slot = [None]

          def ev_fk(t, bi, bk):
              T = t.T
              XN, XNt, SMt = XNs[t.slot % 2], XNts[t.slot % 2], SMts[t.slot]
              if gfk_slot[0] is None:
                  gfk_slot[0] = bpiece(g_fk)
              gi_ = gfk_slot[0]
              z, zt = sc()
              tt(z[0:T, :], PS[bk][0:T, :], BS[bi][0:T, :], ALU.add, [PSt[bk], BSt[bi]], [zt])
              yield
              yield from headnorm_g(z, zt, t, 8, 64, 2)
              tt(z[0:T, :].rearrange("p (h d) -> p h d", h=8), z[0:T, :].rearrange("p (h d) -> p h d", h=8),
                 SM[0:T, t.slot, 2:10].unsqueeze(2).broadcast_to([T, 8, 64]), ALU.mult, [zt, SMt], [zt])
              yield
              tt(z[0:T, :], z[0:T, :], BS[gi_][0:T, :], ALU.mult, [zt, BSt[gi_]], [zt])
              yield
              dst = (o_kp if t.seq.name == "p" else o_ks)
              r0 = t.k0 if t.seq.name == "p" else 0
              kb.dma("sp", dst[r0:r0 + T, :], z[0:T, :], zt, False)
              xo = (t.slot // 2) * 512
              act(XN[0:T, xo:xo + 512], z[0:T, :], AF.Copy, [zt], [XNt])
              yield "XN"
              for q in range(4):
                  tr(PSB[bk][:, q, 0:T], XN[0:T, xo + q * 128:xo + (q + 1) * 128], identb[0:T, 0:T], [XNt, tconst], [PSt[bk]])
              yield
              cp(KT[:, :, t.k0:t.k0 + T], PSB[bk][:, 0:4, 0:T], [PSt[bk]], kt_toks(t.k0, t.k0 + T))
              yield

          def ev_fv(t, bi, bk):
              T = t.T
              z, zt = sc()
              tt(z[0:T, :], PS[bk][0:T, :], BS[bi][0:T, :], ALU.add, [PSt[bk], BSt[bi]], [zt])
              yield
              dst = (o_vp if t.seq.name == "p" else o_vs)
              r0 = t.k0 if t.seq.name == "p" else 0
              kb.dma("sp", dst[r0:r0 + T, :], z[0:T, :], zt, False)
              act(VA[0:T, t.vi, :, 0:64], z[0:T, :].rearrange("p (h d) -> p h d", h=8), AF.Copy, [zt], [KVt[t.vi]])
              yield

          def ev_mv(t, bi, bk):
              T = t.T
              tt(Vm[0:T, t.slot, :, 0:128], PS[bk][0:T, :].rearrange("p (h d) -> p h d", h=4),
                 BS[bi][0:T, :].rearrange("p (h d) -> p h d", h=4), ALU.add, [PSt[bk], BSt[bi]], [U1t])
              yield

          def ev_mo(t, bi, bk):
              T = t.T
              z, zt = sc()
              tt(z[0:T, :], PS[bk][0:T, :], BS[bi][0:T, :], ALU.add, [PSt[bk], BSt[bi]], [zt])
              yield
              act(OG[0:T, t.slot, :], z[0:T, :], AF.Sigmoid, [zt], [U1t])
              yield

          pend_ = tokmajor(0, ev_fq, 0, None, True)
          pend_ = tokmajor(512, ev_fk, 4, pend_, True)
          tokmajor(1024, ev_fv, 0, pend_, False)
          tokmajor(2056, ev_mv, 4)
          tokmajor(2568, ev_mo, 0)

          chk("B1", gi)
          for r in runs:
              cp(mxT[:, :, r[3] - 3:r[3]], r[0].HALO[:], [r[0].halot], [U1t])
          wi = wpiece([(0, 8, 0, 512, ("w_in", 0, 1544))], direct=True)
          for j in range(4):
              bank = j % 4
              for k in range(8):
                  mm(PA[bank][:, 0:N], WS[wi][:, k, j * 128:(j + 1) * 128], hT[:, k, 0:N], k == 0, k == 7,
                     [hTt, WSt[wi]], [PAt[bank]])
              for r in runs:
                  n = r[2] - r[1]
                  act(mxT[:, j, r[3]:r[3] + n], PA[bank][:, r[1]:r[2]], AF.Identity, [PAt[bank], tconst], [U1t],
                      bias=CV[:, C_BFE + j:C_BFE + j + 1])
              for r in runs:
                  n = r[2] - r[1]
                  m0 = r[3]
                  cvt, cvtt = sc()
                  ts(cvt[:, 0:n], mxT[:, j, m0 - 3:m0 - 3 + n], CV[:, C_WCV + 4 * j:C_WCV + 4 * j + 1],
                     CV[:, C_BCV + j:C_BCV + j + 1], ALU.mult, ALU.add, [U1t, tconst], [cvtt])
                  for i in range(1, 4):
                      stt(cvt[:, 0:n], mxT[:, j, m0 - 3 + i:m0 - 3 + i + n],
                          CV[:, C_WCV + 4 * j + i:C_WCV + 4 * j + i + 1], cvt[:, 0:n], ALU.mult, ALU.add,
                          [U1t, tconst, cvtt], [cvtt])
                  act(xcT[:, j, r[1]:r[2]], cvt[:, 0:n], AF.Silu, [cvtt], [xcTt])
          for r in runs:
              n = r[2] - r[1]
              cp(r[0].HALO[:], mxT[:, :, r[3] + n - 3:r[3] + n], [U1t], [r[0].halot])

          chk("B2", gi)
          if gi == 0:
              pc_F()
              pc_Gu()
          chk("B3", gi)
          SQK = float(128 ** -0.5)

          def gen_C():
              for t in tiles:
                  s, T, a = t.seq, t.T, t.c0
                  b = a + T
                  act(WI[:, a:b], NM[:, a:b], AF.Exp, [GRt, s.cart], [GRt], bias=s.car[0:4, 3:4])
                  act(WKr[:, a:b], UU[:, a:b], AF.Exp, [GRt], [GRt], bias=NM[:, b - 1:b])
                  act(DEC_[0:4, t.slot:t.slot + 1], NM[:, b - 1:b], AF.Exp, [GRt, s.cart], [DECt], bias=s.car[0:4, 3:4])
                  ts(s.car[0:4, 3:4], NM[:, b - 1:b], -1.0, None, ALU.mult, None, [GRt], [s.cart])
                  for qi, (src, pb) in enumerate(((UU, 0), (WI, 64), (WKr, 64), (EM, 64))):
                      tr(PS[6][0:T, qi * 4:(qi + 1) * 4], src[:, a:b], identf[pb:pb + 4, pb:pb + 4], [GRt, tconst], [PSt[6]])
                  cp(GC[0:T, t.slot, :], PS[6][0:T, 0:16], [PSt[6]], [GCt])
                  tr(PS[7][0:T, 0:8], CSf[:, a:b], identf[32:40, 32:40], [GRt, tconst], [PSt[7]])
                  tr(PS[7][0:T, 8:16], Lf[:, a:b], identf[32:40, 32:40], [GRt, tconst], [PSt[7]])
                  cp(FN[0:T, t.vi, :], PS[7][0:T, 0:8], [PSt[7]], [KVt[t.vi]])
                  lo_, lot = sc()
                  ts(lo_[0:T, 0:8], PS[7][0:T, 8:16], -1.0, None, ALU.mult, None, [PSt[7]], [lot])
                  dst = (o_lfp if s.name == "p" else o_lfs)
                  r0 = t.k0 if s.name == "p" else 0
                  kb.dma("sp", dst[r0:r0 + T, :], lo_[0:T, 0:8], lot, False)
                  yield
              nt = len(tiles)
              for h in range(4):
                  mm(PS[6][:, 32 + h * 4:32 + h * 4 + nt], sel4[:, h, :], DEC_[0:4, 0:nt], True, True,
                     [DECt, tconst], [PSt[6]])
              cp(DECB[:, :, 0:nt], PS[6][:, 32:48].rearrange("p (h t) -> p h t", h=4)[:, :, 0:nt], [PSt[6]], [DECt])
              yield

              X, Xt, Y, Yt = PS[6], PSt[6], PS[7], PSt[7]
              for t in tiles:
                  s, T, a = t.seq, t.T, t.c0
                  b = a + T
                  for h in range(4):
                      mm(X[:, 0:T], WQ[:, h, :], xcT[:, h, a:b], True, True, [xcTt, tconst], [Xt])
                      mm(X[:, 128:128 + T], WK[:, h, :], xcT[:, h, a:b], True, True, [xcTt, tconst], [Xt])
                      mm(X[0:T, 256:384], xcT[:, h, a:b], WK[:, h, :], True, True, [xcTt, tconst], [Xt])
                      mm(X[0:T, 384:384 + T], sel4[:, h, 0:T], NM[:, a:b], True, True, [GRt, tconst], [Xt])
                      cp(QTs[:, 0:T], X[:, 0:T], [Xt], [mlt["QTs"]])
                      ts(KTs[:, 0:T], X[:, 128:128 + T], SQK, None, ALU.mult, None, [Xt], [mlt["KTs"]])
                      ts(KKs[0:T, :], X[0:T, 256:384], SQK, None, ALU.mult, None, [Xt], [mlt["KKs"]])
                      act(WTs[0:T, 0:T], X[0:T, 384:384 + T], AF.Exp, [Xt, GCt], [mlt["WTs"]], bias=GC[0:T, t.slot, h:h + 1])
                      tt(WTs[0:T, 0:T], WTs[0:T, 0:T], UT[0:T, 0:T], ALU.mult, [mlt["WTs"], tconst], [mlt["WTs"]])
                      ts(VTs[0:T, 0:129], Vm[0:T, t.slot, h, :], GC[0:T, t.slot, 8 + h:9 + h], None, ALU.mult, None,
                         [U1t, GCt], [mlt["VTs"]])
                      yield
                      mm(X[0:T, 384:384 + T], KTs[:, 0:T], QTs[:, 0:T], True, True, [mlt["KTs"], mlt["QTs"]], [Xt])
                      tt(ATs[0:T, 0:T], X[0:T, 384:384 + T], WTs[0:T, 0:T], ALU.mult, [Xt, mlt["WTs"]], [mlt["ATs"]])
                      yield
                      mm(Y[0:T, 128:257], ATs[0:T, 0:T], Vm[0:T, t.slot, h, :], True, True, [mlt["ATs"], U1t], [Yt])
                      mm(Y[0:T, 257:386], QTs[:, 0:T], s.Cb[:, h, :], True, True, [mlt["QTs"], s.Cbt], [Yt])
                      mm(X[:, 0:129], KKs[0:T, :], VTs[0:T, 0:129], True, True, [mlt["KKs"], mlt["VTs"]], [Xt])
                      ts(INs[0:T, 0:129], Y[0:T, 257:386], GC[0:T, t.slot, 4 + h:5 + h], None, ALU.mult, None, [Yt, GCt], [mlt["INs"]])
                      tt(NUMs[0:T, 0:129], Y[0:T, 128:257], INs[0:T, 0:129], ALU.add, [Yt, mlt["INs"]], [mlt["NUMs"]])
                      stt(s.C[:, h, :], s.C[:, h, :], DECB[:, h, t.slot:t.slot + 1], X[:, 0:129], ALU.mult, ALU.add,
                          [s.Ct, DECt, Xt], [s.Ct])
                      cp(s.Cb[:, h, :], s.C[:, h, :], [s.Ct], [s.Cbt])
                      act(NUMs[0:T, 129:130], NUMs[0:T, 128:129], AF.Abs, [mlt["NUMs"]], [mlt["NUMs"]])
                      tt(NUMs[0:T, 130:131], NUMs[0:T, 129:130], GC[0:T, t.slot, 12 + h:13 + h], ALU.max,
                         [mlt["NUMs"], GCt], [mlt["NUMs"]])
                      op("dve", lambda e: e.reciprocal(out=NUMs[0:T, 131:132], in_=NUMs[0:T, 130:131]),
                         [mlt["NUMs"]], [mlt["NUMs"]])
                      stt(HH[0:T, h * 128:(h + 1) * 128], NUMs[0:T, 0:128], NUMs[0:T, 131:132],
                          OG[0:T, t.slot, h * 128:(h + 1) * 128], ALU.mult, ALU.mult, [mlt["NUMs"], U1t], [mlt["HH"]])
                      yield
                  XN, XNt, SMt = XNs[t.slot % 2], XNts[t.slot % 2], SMts[t.slot]
                  headnorm(HH, mlt["HH"], t, 4, 128, 10)
                  tt(XN[0:T, 0:512].rearrange("p (h d) -> p h d", h=4), HH[0:T, :].rearrange("p (h d) -> p h d", h=4),
                     SM[0:T, t.slot, 10:14].unsqueeze(2).broadcast_to([T, 4, 128]), ALU.mult, [mlt["HH"], SMt], [XNt])
                  yield
                  tmp, tmpt = sc()
                  tt(tmp[:, 0:4 * T].rearrange("p (h t) -> p h t", h=4), xcT[:, :, a:b],
                     CV[:, C_SKP:C_SKP + 4].unsqueeze(2).broadcast_to([128, 4, T]), ALU.mult, [xcTt, tconst], [tmpt])
                  bq = 1
                  for q in range(4):
                      tr(TRP[bq][:, q, 0:T], XN[0:T, q * 128:(q + 1) * 128], identb[0:T, 0:T], [XNt, tconst], [TRPt[bq]])
                  for q in range(4):
                      stt(hmT[:, q, a:b], TRP[bq][:, q, 0:T], CV[:, C_GMN + q:C_GMN + q + 1], tmp[:, q * T:(q + 1) * T],
                          ALU.mult, ALU.add, [TRPt[bq], tconst, tmpt], [hmTt])
                  if t.last:
                      pre = "p" if s.name == "p" else "s"
                      oC, on, om, ocv = ((o_Cp, o_np, o_mp, o_cvp) if pre == "p" else (o_Cs, o_ns, o_ms, o_cvs))
                      kb.dma("sp", oC.rearrange("h k v -> k h v"), s.C[:, :, 0:128], s.Ct, False)
                      ncol = 0 if pre == "p" else 4
                      cp(NTs[:, ncol:ncol + 4].unsqueeze(2), s.C[:, :, 128:129], [s.Ct], [NTst])
                      kb.dma("sp", on.rearrange("k h o -> k (h o)"), NTs[:, ncol:ncol + 4], NTst, False)
                      kb.dma("sp", om, s.car[0:4, 2:3], s.cart, False)
                      kb.dma("sp", ocv, s.HALO[:], s.halot, False)
                  yield

          def gen_D():
            for r in runs:
              s, qa, qb = r[0], r[1], r[2]
              Nq = qb - qa
              rt = r[4]
              if s.name == "p":
                  last_idx = rt[-1].idx
                  ktl = []
                  for i in range(last_idx + 1):
                      Tk = NMETA if i == 0 else 128
                      k0 = 0 if i == 0 else NMETA + (i - 1) * 128
                      own = [x for x in rt if x.idx == i]
                      qs = (own[0].c0 - qa) if own else 0
                      ktl.append((k0, Tk, i, qs, bool(own)))
              else:
                  ktl = [(SK0 + j * 128, 128, SV0 + j, 0, False) for j in range(8)]
                  ktl.append((rt[0].k0, rt[0].T, rt[0].vi, 0, True))
              blocks = []
              for jp in range(4):
                  for ki, kt_ in enumerate(ktl):
                      blocks.append((jp, ki) + kt_)
              nkt = len(ktl)
              LA = 1

              def emit_S(i):
                  jp, ki, k0, Tk, vi, qs, diag = blocks[i]
                  ktk = kt_toks(k0, k0 + Tk)
                  for r_ in range(2):
                      rr = r_ * 64
                      Sb, Sbt = PA[2 * (i % 2) + r_], PAt[2 * (i % 2) + r_]
                      mm(Sb[0:Tk, qs:Nq], KT[rr:rr + 64, jp, k0:k0 + Tk], QT[rr:rr + 64, jp, qa + qs:qb], True, False,
                         ktk + [QTt], [Sbt], skip=True)
                  for r_ in range(2):
                      h = 2 * jp + r_
                      Sb, Sbt = PA[2 * (i % 2) + r_], PAt[2 * (i % 2) + r_]
                      if r_ == 0:
                          mm(Sb[0:Tk, qs:Nq], sel8[0:64, h, 0:Tk], FTqP[:, qa + qs:qb], False, not diag,
                             [tconst, QTt], [Sbt], skip=True)
                      else:
                          mm(Sb[0:Tk, qs:Nq], sel8[64:128, h, 0:Tk], FTq2P[:, qa + qs:qb], False, not diag,
                             [tconst, QTt], [Sbt], skip=True)
                  if diag:
                      for r_ in range(2):
                          Sb, Sbt = PA[2 * (i % 2) + r_], PAt[2 * (i % 2) + r_]
                          mm(Sb[0:Tk, qs:qs + Tk], identb[0:Tk, 0:Tk], MASKN[0:Tk, 0:Tk], False, True,
                             [tconst], [Sbt], skip=True)

              def emit_PV(i):
                  jp, ki, k0, Tk, vi, qs, diag = blocks[i]
                  for r_ in range(2):
                      h = 2 * jp + r_
                      Sb, Sbt = PA[2 * (i % 2) + r_], PAt[2 * (i % 2) + r_]
                      pb_, pbt = PTB[2 * (i % 2) + r_], PTBt[2 * (i % 2) + r_]
                      act(pb_[0:Tk, qs:Nq], Sb[0:Tk, qs:Nq], AF.Exp, [Sbt, KVt[vi]], [pbt], bias=FN[0:Tk, vi, h:h + 1])
                  for r_ in range(2):
                      h = 2 * jp + r_
                      pb_, pbt = PTB[2 * (i % 2) + r_], PTBt[2 * (i % 2) + r_]
                      OB, OBt = PM[r_], PMt[r_]
                      vo = (vi * 8 + h) * 65
                      mm(OB[:, qs:Nq], VAf[0:Tk, vo:vo + 128], pb_[0:Tk, qs:Nq], ki == 0, ki == nkt - 1,
                         [KVt[vi], pbt], [OBt], skip=True)

              def emit_norm_pair(jp_, bk0):
                  rds = []
                  for r_ in range(2):
                      OB, OBt = PM[r_], PMt[r_]
                      rd_, rdt = sc()
                      act(rd_[64:65, 0:Nq], OB[64:65, 0:Nq], AF.Ln, [OBt], [rdt])
                      act(rd_[64:65, 0:Nq], rd_[64:65, 0:Nq], AF.Exp, [rdt], [rdt], scale=-1.0)
                      rds.append((rd_, rdt))
                  for r_ in range(2):
                      rd_, rdt = rds[r_]
                      bk = bk0 + r_
                      mm(PA[bk][0:64, 0:Nq], ONESF[64:65, 0:64], rd_[64:65, 0:Nq], True, True, [rdt, tconst], [PAt[bk]])
                  for r_ in range(2):
                      h = 2 * jp_ + r_
                      j, rr = h // 2, (h % 2) * 64
                      OB, OBt = PM[r_], PMt[r_]
                      bk = bk0 + r_
                      rb, rbt = rds[r_]
                      act(rb[0:64, 0:Nq], PA[bk][0:64, 0:Nq], AF.Copy, [PAt[bk]], [rbt])
                      tt(aT[rr:rr + 64, j, qa:qb], OB[0:64, 0:Nq], rb[0:64, 0:Nq], ALU.mult, [OBt, rbt], [aTt])

              nb = len(blocks)
              for i in range(nb + LA):
                  if i < nb:
                      emit_S(i)
                  jb = i - LA
                  if jb >= 0:
                      emit_PV(jb)
                      if blocks[jb][1] == nkt - 1:
                          emit_norm_pair(blocks[jb][0], 2 * (jb % 2))
                  yield

          nC = sum(13 for t in tiles) + 2 * len(tiles) + 1
          nD = 0
          for r in runs:
              nk_ = (r[4][-1].idx + 1) if r[0].name == "p" else 9
              nD += 4 * nk_ + 1
          gC, gD = gen_C(), gen_D()
          if SEQ_CD:
              for _ in gC:
                  pass
              for _ in gD:
                  pass
          npre = len(tiles) + 1
          pre_rate = npre if gi == 0 else 2
          doneC = doneD = 0
          aliveC = aliveD = True
          while aliveC or aliveD:
              if aliveD:
                  try:
                      next(gD)
                      doneD += 1
                  except StopIteration:
                      aliveD = False
              if aliveC:
                  remD = max(nD - doneD, 0)
                  if doneC < npre:
                      want = min(pre_rate, npre - doneC)
                  elif remD == 0 or not aliveD:
                      want = max(nC - doneC, 1)
                  else:
                      want = -(-(nC - doneC) // (remD + 1))
                  for _ in range(want):
                      try:
                          next(gC)
                          doneC += 1
                      except StopIteration:
                          aliveC = False
                          break

          chk("D", gi)
          if gi == 0:
              pc_Gd()
          for cb in range(4):
              c0 = cb * 256
              wa = wpiece([(0, 8, 0, 256, ("w_in", 0, 3088 + c0)), (0, 8, 256, 256, ("w_in", 0, 4112 + c0))])
              wb = wpiece([(0, 4, 0, 256, ("w_fp", 0, c0)), (4, 4, 0, 256, ("w_mp", 0, c0))])
              for jj in range(2):
                  f = cb * 2 + jj
                  cs = slice(jj * 128, (jj + 1) * 128)
                  cs2 = slice(256 + jj * 128, 256 + (jj + 1) * 128)
                  bks = (0, 1, 2, 3) if f % 2 == 0 else (4, 5, 6, 7)
                  Ea, Eat, Eb, Ebt = PS[bks[0]], PSt[bks[0]], PS[bks[1]], PSt[bks[1]]
                  Ec, Ect, Ed, Edt = PS[bks[2]], PSt[bks[2]], PS[bks[3]], PSt[bks[3]]
                  for k in range(8):
                      mm(Ea[:, 0:N], WS[wa][:, k, cs], hT[:, k, 0:N], k == 0, k == 7, [hTt, WSt[wa]], [Eat])
                  for k in range(8):
                      mm(Ec[:, 0:N], WS[wa][:, k, cs2], hT[:, k, 0:N], k == 0, k == 7, [hTt, WSt[wa]], [Ect])
                  for k in range(4):
                      mm(Eb[:, 0:N], WS[wb][:, k, cs], aT[:, k, 0:N], k == 0, k == 3, [aTt, WSt[wb]], [Ebt])
                  for k in range(4):
                      mm(Ed[:, 0:N], WS[wb][:, 4 + k, cs], hmT[:, k, 0:N], k == 0, k == 3, [hmTt, WSt[wb]], [Edt])
                  sa, sat = sc()
                  act(sa[:, 0:N], Ea[:, 0:N], AF.Sigmoid, [Eat, tconst], [sat], bias=CV[:, C_BFE + 4 + f:C_BFE + 5 + f])
                  t1, t1t = sc()
                  tt(t1[:, 0:N], sa[:, 0:N], Eb[:, 0:N], ALU.mult, [sat, Ebt], [t1t])
                  sb_, sbt = sc()
                  act(sb_[:, 0:N], Ec[:, 0:N], AF.Sigmoid, [Ect, tconst], [sbt], bias=CV[:, C_BFE + 12 + f:C_BFE + 13 + f])
                  t2, t2t = sc()
                  tt(t2[:, 0:N], sb_[:, 0:N], Ed[:, 0:N], ALU.mult, [sbt, Edt], [t2t])
                  tt(mgT[:, f, 0:N], t1[:, 0:N], t2[:, 0:N], ALU.add, [t1t, t2t], [mgTt])

          chk("E", gi)
          ftiles = [t for t in tiles if t.ffn]

          def proj_norm_res(pieces_for_c, lhs_of, nk, gvec, lhst, after_tile=None):
              gsl = []
              for c_ in range(2):
                  gsl.append(bpiece(gvec[c_ * 512:(c_ + 1) * 512]))
                  kc = 0
                  for (n_, src) in pieces_for_c(c_):
                      wi_ = wpiece([(0, n_, 0, 512, src)])
                      for k in range(n_):
                          for t in ftiles:
                              bk = 4 * c_ + t.slot
                              mm(PS[bk][0:t.T, :], lhs_of(kc, t), WS[wi_][:, k, :], kc == 0, kc == nk - 1,
                                 list(lhst) + [WSt[wi_]], [PSt[bk]])
                          kc += 1
                  for t in ftiles:
                      T = t.T
                      bk = 4 * c_ + t.slot
                      XN, XNt, SMt = XNs[t.slot % 2], XNts[t.slot % 2], SMts[t.slot]
                      act(XN[0:T, 0:512], PS[bk][0:T, :], AF.Square, [PSt[bk]], [XNt, SMt],
                          accum_out=SM[0:T, t.slot, 14 + c_:15 + c_])
              for t in ftiles:
                  T = t.T
                  SMt = SMts[t.slot]
                  rc = SM[0:T, t.slot, 16:17]
                  tt(rc, SM[0:T, t.slot, 14:15], SM[0:T, t.slot, 15:16], ALU.add, [SMt], [SMt])
                  ts(rc, rc, 1.0 / D, EPS, ALU.mult, ALU.add, [SMt], [SMt])
              for t in ftiles:
                  rc = SM[0:t.T, t.slot, 16:17]
                  act(rc, rc, AF.Ln, [SMts[t.slot]], [SMts[t.slot]])
              for t in ftiles:
                  rc = SM[0:t.T, t.slot, 16:17]
                  act(rc, rc, AF.Exp, [SMts[t.slot]], [SMts[t.slot]], scale=-0.5)
              for t in ftiles:
                  T = t.T
                  SMt = SMts[t.slot]
                  rc = SM[0:T, t.slot, 16:17]
                  for c_ in range(2):
                      bk = 4 * c_ + t.slot
                      tmp, tmpt = sc()
                      stt(tmp[0:T, :], PS[bk][0:T, :], rc, BS[gsl[c_]][0:T, :], ALU.mult, ALU.mult,
                          [PSt[bk], SMt, BSt[gsl[c_]]], [tmpt])
                      tt(XT[t.slot][0:T, c_ * 512:(c_ + 1) * 512], XT[t.slot][0:T, c_ * 512:(c_ + 1) * 512], tmp[0:T, :],
                         ALU.add, [tmpt, XTt[t.slot]], [XTt[t.slot]], eng="pool")
                  if after_tile is not None:
                      after_tile(t)

          if ftiles:
              chk("Epre", gi)
              warm_ln()
              proj_norm_res(lambda c_: [(8, ("w_out", 0, c_ * 512))],
                            lambda kc, t: mgT[:, kc, t.c0:t.c0 + t.T], 8, g_pm, [mgTt])
              for i0 in range(0, len(ftiles), 2):
                  rr([norm_to_T(XT[t.slot], XTt[t.slot], t, C_GPF, hT, hTt) for t in ftiles[i0:i0 + 2]])

              chk("F", gi)
              if gi + 1 < len(groups):
                  for t2 in groups[gi + 1]:
                      if t2.seq.name == "p" and t2.idx >= 1:
                          sl = groups[gi + 1].index(t2)
                          kb.dma("sp", STG[sl][0:t2.T, :], xp[(t2.idx - 1) * 128:t2.idx * 128, :], STGt[sl], True)
                          staged[(gi + 1, sl)] = True
              fa, fb = ftiles[0].c0, ftiles[-1].c0 + ftiles[-1].T
              Nf = fb - fa
              bankp = [(PA[0], PAt[0], PA[1], PAt[1]), (PA[2], PAt[2], PA[3], PAt[3]), (PM[0], PMt[0], PM[1], PMt[1])]
              for st in range(11):
                  wi = wpiece([(0, 8, 0, 256, ("w_gu", 0, st * 256)), (0, 8, 256, 256, ("w_gu", 0, DFF + st * 256))])
                  for jj in range(2):
                      f = st * 2 + jj
                      G_, Gt_, U_, Ut_ = bankp[f % 3]
                      for k in range(8):
                          mm(G_[:, 0:Nf], WS[wi][:, k, jj * 128:(jj + 1) * 128], hT[:, k, fa:fb], k == 0, k == 7,
                             [hTt, WSt[wi]], [Gt_])
                      for k in range(8):
                          mm(U_[:, 0:Nf], WS[wi][:, k, 256 + jj * 128:256 + (jj + 1) * 128], hT[:, k, fa:fb],
                             k == 0, k == 7, [hTt, WSt[wi]], [Ut_])
                      sg, sgt = sc()
                      act(sg[:, 0:Nf], G_[:, 0:Nf], AF.Silu, [Gt_], [sgt])
                      tt(hidT[:, f, fa:fb], sg[:, 0:Nf], U_[:, 0:Nf], ALU.mult, [sgt, Ut_], [U1t, QTt])
              warm_ln()

              def store_y(t):
                  if t.seq.name == "p":
                      dst = o_yp[(t.idx - 1) * 128:t.idx * 128, :]
                  else:
                      dst = o_ys
                  kb.dma("sp", dst, XT[t.slot][0:t.T, :], XTt[t.slot], False)

              proj_norm_res(lambda c_: [(8, ("w_dn", 0, c_ * 512)), (8, ("w_dn", 1024, c_ * 512)),
                                        (6, ("w_dn", 2048, c_ * 512))],
                            lambda kc, t: hidT[:, kc, t.c0:t.c0 + t.T], 22, g_pf, [U1t, QTt], after_tile=store_y)
          if gi + 1 < len(groups):
              memset(Vm[:, :, :, 128:129], 1.0, [U1t], eng="dve")

    except StopBuild:
        pass
    kb.finish()
    kb.sbuf_left = nc.sbuf_bytes_remaining
    es.close()
    return nc, kb


def host_inputs(inp, i, npt=32):
    f = lambda a: np.ascontiguousarray(a, dtype=np.float32)
    w_in = inp["w_in"][0]
    b_in = inp["b_in"][0]
    cvec = np.zeros((128, 68), np.float32)
    cvec[:, 0:8] = inp["g_pre_mix"][0].reshape(8, 128).T
    cvec[:, 8:16] = inp["g_pre_ffn"][0].reshape(8, 128).T
    cvec[:, 16:20] = inp["g_fq"][0].reshape(4, 128).T
    cvec[:, 20:24] = b_in[1544:2056].reshape(4, 128).T
    cvec[:, 24:32] = b_in[3088:4112].reshape(8, 128).T
    cvec[:, 32:40] = b_in[4112:5136].reshape(8, 128).T
    cvec[:, 40:44] = inp["g_mnorm"][0].reshape(4, 128).T
    cvec[:, 44:48] = inp["skip_m"][0].reshape(4, 128).T
    cvec[:, 48:52] = inp["b_conv"][0].reshape(4, 128).T
    cvec[:, 52:68] = inp["w_conv"][0].reshape(4, 4, 128).transpose(2, 1, 0).reshape(128, 16)
    gbias = np.zeros((8, 3), np.float32)
    gbias[:, 0] = b_in[1536:1544]
    gbias[0:4, 1] = b_in[3080:3084]
    gbias[0:4, 2] = b_in[3084:3088]
    return {
        "xp": f(inp["x_prompt"][i][:npt * 128]), "xs": f(inp["x_sample"][i]), "meta": f(inp["meta_tokens"]),
        "ck": f(inp["cache_fox_k"][0, i].reshape(PAST, 512)), "cv": f(inp["cache_fox_v"][0, i].reshape(PAST, 512)),
        "clf": f(inp["cache_fox_logf"][0, i]),
        "sC": f(inp["state_mlstm_C"][0, i]), "snT": f(inp["state_mlstm_n"][0, i].T),
        "sm": f(inp["state_mlstm_m"][0, i].reshape(4, 1)),
        "sconvT": f(inp["state_mlstm_conv"][0, i].reshape(3, 4, 128).transpose(2, 1, 0)),
        "w_in": f(w_in), "b_in": f(b_in),
        "w_g16": f(np.concatenate([w_in[:, 1536:1544], w_in[:, 3080:3088]], axis=1)),
        "gbias": gbias, "cvec": cvec,
        "g_fk": f(inp["g_fk"][0].reshape(512)), "g_pm": f(inp["g_post_mix"][0]), "g_pf": f(inp["g_post_ffn"][0]),
        "w_mq": f(inp["w_mq"][0]), "w_mk": f(inp["w_mk"][0]),
        "w_fp": f(inp["w_fox_proj"][0]), "w_mp": f(inp["w_ml_proj"][0]), "w_out": f(inp["w_out"][0]),
        "w_gu": f(inp["w_gate_up"][0]), "w_dn": f(inp["w_down"][0]),
    }


def assemble(results, npt=32):
    n = len(results)
    LP = NMETA + npt * 128
    st = lambda k: np.stack([np.asarray(r[k], dtype=np.float32) for r in results])
    conv = lambda k: st(k).transpose(0, 3, 2, 1).reshape(n, 3, 512)
    return (
        st("o_yp"), st("o_ys"),
        st("o_kp").reshape(1, n, LP, 8, 64), st("o_vp").reshape(1, n, LP, 8, 64), st("o_lfp").reshape(1, n, LP, 8),
        st("o_Cp").reshape(1, n, 4, 128, 128), st("o_np").reshape(n, 128, 4).transpose(0, 2, 1).reshape(1, n, 4, 128),
        st("o_mp").reshape(1, n, 4), conv("o_cvp").reshape(1, n, 3, 512),
        st("o_ks").reshape(1, n, DEC, 8, 64), st("o_vs").reshape(1, n, DEC, 8, 64), st("o_lfs").reshape(1, n, DEC, 8),
        st("o_Cs").reshape(1, n, 4, 128, 128), st("o_ns").reshape(n, 128, 4).transpose(0, 2, 1).reshape(1, n, 4, 128),
        st("o_ms").reshape(1, n, 4), conv("o_cvs").reshape(1, n, 3, 512),
    )


_NC_CACHE = {}


def kernel(**inputs):
    inp = {k: np.asarray(v) for k, v in inputs.items()}
    if 32 not in _NC_CACHE:
        _NC_CACHE[32] = build(32)[0]
    nc = _NC_CACHE[32]
    in_maps = [host_inputs(inp, i, 32) for i in range(8)]
    res = run_bass_kernel_spmd(nc, in_maps, core_ids=list(range(8)))
    return assemble(res.results, 32)
```
